# Optimizing a Trainium2 kernel written in Bass

```python
import math
import jax, jax.numpy as jnp
from jax import lax
import numpy as np

D_MODEL = 1024
BATCH = 8
SEQ = 8192
DEPTH = 2

MIX_WIDTH = D_MODEL
ATTN_WIDTH = MIX_WIDTH // 2
SSM_WIDTH = MIX_WIDTH - ATTN_WIDTH
DIFF_HEAD_DIM = 64
DIFF_V_DIM = 2 * DIFF_HEAD_DIM
DIFF_N_HEADS = ATTN_WIDTH // DIFF_V_DIM
QK_WIDTH = DIFF_N_HEADS * 2 * DIFF_HEAD_DIM
V_WIDTH = DIFF_N_HEADS * DIFF_V_DIM
Q_BLOCK = 128
ROPE_THETA = 10000.0
SSM_GROUP = 16
SSM_N_GROUPS = SSM_WIDTH // SSM_GROUP
SSM_STATE = 64
DT_MIN = 0.001
DT_MAX = 0.1
IN_WIDTH = 2 * QK_WIDTH + V_WIDTH + SSM_WIDTH
D_FF = 4 * D_MODEL
ALPHA = (2.0 * DEPTH) ** 0.25
BETA = (8.0 * DEPTH) ** -0.25
LN_EPS = 1e-5
RMS_EPS = 1e-5

kernel_name = "hybrid_diffattn_s5_deepnorm"


def layer_norm(x, g, b):
    xf = x.astype(jnp.float32)
    mu = jnp.mean(xf, axis=-1, keepdims=True)
    var = jnp.mean(jnp.square(xf - mu), axis=-1, keepdims=True)
    y = (xf - mu) * lax.rsqrt(var + LN_EPS) * g.astype(jnp.float32) + b.astype(jnp.float32)
    return y.astype(x.dtype)


def rms_norm(x, g):
    xf = x.astype(jnp.float32)
    y = xf * lax.rsqrt(jnp.mean(jnp.square(xf), axis=-1, keepdims=True) + RMS_EPS) * g.astype(jnp.float32)
    return y.astype(x.dtype)


def rope_tables(seq_len):
    pos = jnp.arange(seq_len, dtype=jnp.float32)
    inv_freq = ROPE_THETA ** (-jnp.arange(0, DIFF_HEAD_DIM, 2, dtype=jnp.float32) / DIFF_HEAD_DIM)
    ang = pos[:, None] * inv_freq[None, :]
    return jnp.cos(ang), jnp.sin(ang)


def apply_rope(x, cos, sin):
    xf = x.astype(jnp.float32)
    x1, x2 = jnp.split(xf, 2, axis=-1)
    out = jnp.concatenate([x1 * cos - x2 * sin, x2 * cos + x1 * sin], axis=-1)
    return out.astype(x.dtype)


def diff_attention(q, k, v, lam, lam_init, subln_g, cos, sin):
    bsz, seq_len, _ = q.shape
    q = q.reshape(bsz, seq_len, DIFF_N_HEADS, 2, DIFF_HEAD_DIM).transpose(0, 2, 3, 1, 4)
    k = k.reshape(bsz, seq_len, DIFF_N_HEADS, 2, DIFF_HEAD_DIM).transpose(0, 2, 3, 1, 4)
    v = v.reshape(bsz, seq_len, DIFF_N_HEADS, DIFF_V_DIM).transpose(0, 2, 1, 3)
    q = apply_rope(q, cos, sin)
    k = apply_rope(k, cos, sin)
    scale = 1.0 / math.sqrt(DIFF_HEAD_DIM)
    n_blocks = -(-seq_len // Q_BLOCK)
    outs = []
    for i in range(n_blocks):
        q0 = i * Q_BLOCK
        kend = min(q0 + Q_BLOCK, seq_len)
        qb = q[:, :, :, q0:kend]
        kb = k[:, :, :, :kend]
        vb = v[:, :, :kend]
        s = jnp.einsum('bhmqd,bhmkd->bhmqk', qb, kb).astype(jnp.float32) * scale
        qpos = q0 + jnp.arange(kend - q0)
        kpos = jnp.arange(kend)
        mask = kpos[None, :] <= qpos[:, None]
        s = jnp.where(mask, s, -jnp.inf)
        p = jax.nn.softmax(s, axis=-1)
        a = p[:, :, 0] - lam * p[:, :, 1]
        outs.append(jnp.einsum('bhqk,bhkv->bhqv', a.astype(vb.dtype), vb))
    o = jnp.concatenate(outs, axis=2)
    o = rms_norm(o, subln_g) * (1.0 - lam_init)
    return o.transpose(0, 2, 1, 3).reshape(bsz, seq_len, ATTN_WIDTH)


def _scan_op(e_i, e_j):
    a_i, b_i = e_i
    a_j, b_j = e_j
    return a_j * a_i, a_j * b_i + b_j


def s5_ssm(u, lam_re, lam_im, log_dt, b_re, b_im, c_re, c_im, d_skip, glu_w, glu_b):
    bsz, seq_len, _ = u.shape
    f32 = jnp.float32
    uf = u.astype(f32).reshape(bsz, seq_len, SSM_N_GROUPS, SSM_GROUP)
    lam_c = lax.complex(lam_re.astype(f32), lam_im.astype(f32))
    dt = jnp.exp(log_dt.astype(f32))[:, None]
    a_bar = jnp.exp(lam_c * dt)
    b_c = lax.complex(b_re.astype(f32), b_im.astype(f32))
    b_bar = ((a_bar - 1.0) / lam_c)[..., None] * b_c
    bu = jnp.einsum('blgc,gpc->blgp', uf.astype(jnp.complex64), b_bar)
    a_seq = jnp.broadcast_to(a_bar[None, None], (1, seq_len) + a_bar.shape)
    _, states = lax.associative_scan(_scan_op, (a_seq, bu), axis=1)
    c_c = lax.complex(c_re.astype(f32), c_im.astype(f32))
    y = jnp.real(jnp.einsum('blgp,gcp->blgc', states, c_c)) + d_skip.astype(f32) * uf
    y = jax.nn.gelu(y.reshape(bsz, seq_len, SSM_WIDTH))
    y = y * jax.nn.sigmoid(y @ glu_w.astype(f32) + glu_b.astype(f32))
    return y.astype(u.dtype)


def setup_inputs(seed: int = 0) -> dict:
    key = jax.random.key(seed)
    ks = jax.random.split(key, 24)
    f32 = jnp.float32
    nrm = lambda k, shape, s: jax.random.normal(k, shape, f32) * s
    G, P, C = SSM_N_GROUPS, SSM_STATE, SSM_GROUP
    n_idx = jnp.arange(P, dtype=f32)
    return {
        "x": nrm(ks[0], (BATCH, SEQ, D_MODEL), 1.0),
        "w_in": nrm(ks[1], (DEPTH, D_MODEL, IN_WIDTH), D_MODEL ** -0.5),
        "w_out": nrm(ks[2], (DEPTH, MIX_WIDTH, D_MODEL), BETA * MIX_WIDTH ** -0.5),
        "lam_qk": nrm(ks[3], (DEPTH, 4, DIFF_HEAD_DIM), 0.1),
        "subln_g": 1.0 + nrm(ks[4], (DEPTH, DIFF_V_DIM), 0.02),
        "ssm_lam_re": -0.5 + nrm(ks[5], (DEPTH, G, P), 0.01),
        "ssm_lam_im": jnp.pi * n_idx + nrm(ks[6], (DEPTH, G, P), 0.01),
        "ssm_log_dt": jax.random.uniform(ks[7], (DEPTH, G), f32, math.log(DT_MIN), math.log(DT_MAX)),
        "ssm_b_re": nrm(ks[8], (DEPTH, G, P, C), (2.0 * C) ** -0.5),
        "ssm_b_im": nrm(ks[9], (DEPTH, G, P, C), (2.0 * C) ** -0.5),
        "ssm_c_re": nrm(ks[10], (DEPTH, G, C, P), 2.0 ** -0.5),
        "ssm_c_im": nrm(ks[11], (DEPTH, G, C, P), 2.0 ** -0.5),
        "ssm_d": nrm(ks[12], (DEPTH, G, C), 1.0),
        "glu_w": nrm(ks[13], (DEPTH, SSM_WIDTH, SSM_WIDTH), SSM_WIDTH ** -0.5),
        "glu_b": nrm(ks[14], (DEPTH, SSM_WIDTH), 0.01),
        "ln1_g": 1.0 + nrm(ks[15], (DEPTH, D_MODEL), 0.02),
        "ln1_b": nrm(ks[16], (DEPTH, D_MODEL), 0.02),
        "w_ff1": nrm(ks[17], (DEPTH, D_MODEL, D_FF), D_MODEL ** -0.5),
        "w_ff2": nrm(ks[18], (DEPTH, D_FF, D_MODEL), BETA * D_FF ** -0.5),
        "ln2_g": 1.0 + nrm(ks[19], (DEPTH, D_MODEL), 0.02),
        "ln2_b": nrm(ks[20], (DEPTH, D_MODEL), 0.02),
    }


def reference(x, w_in, w_out, lam_qk, subln_g, ssm_lam_re, ssm_lam_im, ssm_log_dt,
              ssm_b_re, ssm_b_im, ssm_c_re, ssm_c_im, ssm_d, glu_w, glu_b,
              ln1_g, ln1_b, w_ff1, w_ff2, ln2_g, ln2_b):
    seq_len = x.shape[1]
    cos, sin = rope_tables(seq_len)
    splits = [QK_WIDTH, 2 * QK_WIDTH, 2 * QK_WIDTH + V_WIDTH]
    for l in range(DEPTH):
        lam_init = 0.8 - 0.6 * math.exp(-0.3 * l)
        lq = lam_qk[l].astype(jnp.float32)
        lam = jnp.exp(jnp.sum(lq[0] * lq[1])) - jnp.exp(jnp.sum(lq[2] * lq[3])) + lam_init
        h = x @ w_in[l]
        q, k, v, u = jnp.split(h, splits, axis=-1)
        attn_out = diff_attention(q, k, v, lam, lam_init, subln_g[l], cos, sin)
        ssm_out = s5_ssm(u, ssm_lam_re[l], ssm_lam_im[l], ssm_log_dt[l], ssm_b_re[l], ssm_b_im[l],
                         ssm_c_re[l], ssm_c_im[l], ssm_d[l], glu_w[l], glu_b[l])
        mix = jnp.concatenate([attn_out, ssm_out], axis=-1) @ w_out[l]
        x = layer_norm(ALPHA * x + mix, ln1_g[l], ln1_b[l])
        ff = jnp.square(jax.nn.relu(x @ w_ff1[l])) @ w_ff2[l]
        x = layer_norm(ALPHA * x + ff, ln2_g[l], ln2_b[l])
    return x
```

```python
import math
from contextlib import ExitStack

import numpy as np
import ml_dtypes

import concourse.bass as bass
import concourse.mybir as mybir
from concourse.bass_utils import run_bass_kernel_spmd

F32 = mybir.dt.float32
BF16 = mybir.dt.bfloat16
AF = mybir.ActivationFunctionType
ALU = mybir.AluOpType
AX = mybir.AxisListType

D_MODEL = 1024
DEPTH = 2
N_HEADS = 4
D_FF = 4096
ALPHA = (2.0 * DEPTH) ** 0.25
LN_EPS = 1e-5
RMS_EPS = 1e-5
ROPE_THETA = 10000.0


class Ev:
    __slots__ = ("sem", "val")

    def __init__(self, sem, val):
        self.sem = sem
        self.val = val


class Prog:
    ENGS = ("pe", "act", "dve", "pool", "sp")

    def __init__(self, nc, stack, n_dma_sems=24):
        self.nc = nc
        self.q = {e: [] for e in self.ENGS}
        self.esem = {}
        self.tick = {}
        for e in ("pe", "act", "dve", "pool"):
            self.esem[e] = stack.enter_context(nc.semaphore("sem_" + e))
            self.tick[e] = 0
        self.dsem = {}
        self.dcnt = {}
        self.drr = {}
        for qn in ("sp", "pool", "act"):
            n = n_dma_sems if qn == "sp" else 8
            self.dsem[qn] = [stack.enter_context(nc.semaphore("dq_%s_%d" % (qn, i))) for i in range(n)]
            self.dcnt[qn] = [0] * n
            self.drr[qn] = 0
        self.seen = {e: {} for e in self.ENGS}
        self.last_w = {}
        self.readers = {}
        self.all_sems = {}

    def _deps(self, reads, writes, deps):
        out = list(deps)
        for k in reads:
            ev = self.last_w.get(k)
            if ev is not None:
                out.append(ev)
        for k in writes:
            ev = self.last_w.get(k)
            if ev is not None:
                out.append(ev)
            out.extend(self.readers.get(k, ()))
        return out

    def _update(self, reads, writes, ev):
        for k in reads:
            self.readers.setdefault(k, []).append(ev)
        for k in writes:
            self.last_w[k] = ev
            self.readers[k] = []

    def _emit_waits(self, eng, deps, skip_sem=None):
        seen = self.seen[eng]
        need = {}
        for ev in deps:
            if ev is None:
                continue
            if skip_sem is not None and ev.sem is skip_sem:
                continue
            sid = id(ev.sem)
            if seen.get(sid, 0) >= ev.val:
                continue
            if sid not in need or need[sid].val < ev.val:
                need[sid] = ev
        for sid, ev in need.items():
            seen[sid] = ev.val
            self.q[eng].append(("wait", ev.sem, ev.val))

    def op(self, eng, fn, reads=(), writes=(), deps=(), sig=True, **kw):
        if isinstance(fn, str):
            name = fn
            fn = lambda e, name=name, kw=kw: getattr(e, name)(**kw)
        d = self._deps(reads, writes, deps)
        self._emit_waits(eng, d, skip_sem=self.esem[eng] if eng == "pe" else None)
        if sig:
            self.tick[eng] += 1
            ev = Ev(self.esem[eng], self.tick[eng])
            self.q[eng].append(("op", fn, self.esem[eng], 1))
        else:
            ev = Ev(self.esem[eng], self.tick[eng] + 1)
            self.q[eng].append(("op", fn, None, 0))
        self._update(reads, writes, ev)
        return ev

    def dma(self, qn, out, in_, reads=(), writes=(), deps=(), **kw):
        d = self._deps(reads, writes, deps)
        i = self.drr[qn]
        self.drr[qn] = (i + 1) % len(self.dsem[qn])
        sem = self.dsem[qn][i]
        prev = self.dcnt[qn][i]
        if prev:
            d.append(Ev(sem, prev))
        self._emit_waits(qn, d)
        self.dcnt[qn][i] = prev + 16
        ev = Ev(sem, prev + 16)
        self.q[qn].append(("dma", out, in_, sem, kw))
        self._update(reads, writes, ev)
        return ev

    def barrier(self):
        evs = []
        for e in ("pe", "act", "dve", "pool"):
            if self.tick[e]:
                evs.append(Ev(self.esem[e], self.tick[e]))
        for qn in ("sp", "pool", "act"):
            for s, c in zip(self.dsem[qn], self.dcnt[qn]):
                if c:
                    evs.append(Ev(s, c))
        for e in self.ENGS:
            self._emit_waits(e, evs)

    def flush(self):
        nc = self.nc
        q = self.q

        def replay(eng, items):
            for it in items:
                if it[0] == "wait":
                    eng.wait_ge(it[1], it[2])
                elif it[0] == "op":
                    ins = it[1](eng)
                    if it[2] is not None:
                        ins.then_inc(it[2], it[3])
                else:
                    _, out, in_, sem, kw = it
                    eng.dma_start(out=out, in_=in_, **kw).then_inc(sem, 16)

        with nc.Block() as blk:
            @blk.tensor
            def _(e):
                replay(e, q["pe"])

            @blk.scalar
            def _(e):
                replay(e, q["act"])

            @blk.vector
            def _(e):
                replay(e, q["dve"])

            @blk.gpsimd
            def _(e):
                replay(e, q["pool"])

            @blk.sync
            def _(e):
                replay(e, q["sp"])
        self.q = {e: [] for e in self.ENGS}


def make_consts(L):
    pos = np.arange(L, dtype=np.float32)
    inv_freq = (ROPE_THETA ** (-np.arange(0, 64, 2, dtype=np.float32) / 64)).astype(np.float32)
    ang = (pos[:, None] * inv_freq[None, :]).astype(np.float32)
    cos = np.cos(ang.astype(np.float64)).astype(np.float32)
    sin = np.sin(ang.astype(np.float64)).astype(np.float32)
    cc = np.concatenate([cos, cos], axis=1)
    ss = np.concatenate([-sin, sin], axis=1)
    rope = np.stack([cc, ss], axis=1).astype(np.float32)
    ident = np.eye(128, dtype=np.float32)
    tri = (np.arange(128)[:, None] <= np.arange(128)[None, :]).astype(np.float32)
    ones = np.ones((128, 128), dtype=np.float32)
    misc = np.stack([ident, tri, ones], axis=0)
    return {"c_rope": rope, "c_misc": misc}


class Ctx:
    pass


def declare_dram(nc, L, dbg=()):
    C = Ctx()
    C.L = L

    def inp(name, shape, dt=F32):
        return nc.dram_tensor(name, list(shape), dt, kind="ExternalInput").ap()

    def scr(name, shape, dt):
        kind = "ExternalOutput" if name in dbg else "Internal"
        return nc.dram_tensor(name, list(shape), dt, kind=kind).ap()

    C.x = inp("x", [L, D_MODEL])
    C.w_in = inp("w_in", [DEPTH, 1024, 2048])
    C.w_out = inp("w_out", [DEPTH, 1024, 1024])
    C.lam_qk = inp("lam_qk", [DEPTH, 4, 64])
    C.subln_g = inp("subln_g", [DEPTH, 128])
    C.ssm_lam_re = inp("ssm_lam_re", [DEPTH, 32, 64])
    C.ssm_lam_im = inp("ssm_lam_im", [DEPTH, 32, 64])
    C.ssm_log_dt = inp("ssm_log_dt", [DEPTH, 32])
    C.ssm_b_re = inp("ssm_b_re", [DEPTH, 32, 64, 16])
    C.ssm_b_im = inp("ssm_b_im", [DEPTH, 32, 64, 16])
    C.ssm_c_re = inp("ssm_c_re", [DEPTH, 32, 16, 64])
    C.ssm_c_im = inp("ssm_c_im", [DEPTH, 32, 16, 64])
    C.ssm_d = inp("ssm_d", [DEPTH, 32, 16])
    C.glu_w = inp("glu_w", [DEPTH, 512, 512])
    C.glu_b = inp("glu_b", [DEPTH, 512])
    C.ln1_g = inp("ln1_g", [DEPTH, 1024])
    C.ln1_b = inp("ln1_b", [DEPTH, 1024])
    C.w_ff1 = inp("w_ff1", [DEPTH, 1024, 4096])
    C.w_ff2 = inp("w_ff2", [DEPTH, 4096, 1024])
    C.ln2_g = inp("ln2_g", [DEPTH, 1024])
    C.ln2_b = inp("ln2_b", [DEPTH, 1024])
    C.c_rope = inp("c_rope", [L, 2, 64])
    C.c_misc = inp("c_misc", [3, 128, 128])
    C.y = nc.dram_tensor("y", [L, D_MODEL], F32, kind="ExternalOutput").ap()
    C.qT = scr("s_qT", [4, 128, L], BF16)
    C.kT = scr("s_kT", [4, 128, L], BF16)
    C.v = scr("s_v", [L, 512], BF16)
    C.uT = scr("s_uT", [512, L], BF16)
    C.mixT = scr("s_mixT", [1024, L], BF16)
    C.x1 = scr("s_x1", [L, 1024], F32)
    C.x1T = scr("s_x1T", [1024, L], BF16)
    C.xs = scr("s_xs", [L, 1024], F32)
    C.gT = scr("s_gT", [512, L], BF16)
    return C


def phase_A(P, nc, C, l, x_src):
    L = C.L
    NMT = L // 512
    NT = L // 128
    ps = C.ps
    psb = C.psb
    with ExitStack() as st:
        sb = lambda n, s, d: st.enter_context(nc.sbuf_tensor("%s_l%d" % (n, l), s, d))
        W = sb("A_w", [128, 8, 2048], BF16)
        rope = sb("A_rope", [128, NT, 2, 64], F32)
        xb = [sb("A_xb%d" % i, [128, 4, 1024], BF16) for i in range(2)]
        xT = [sb("A_xT%d" % i, [128, 8, 512], BF16) for i in range(2)]
        uo = [sb("A_uo%d" % i, [128, 4, 512], BF16) for i in range(2)]
        vo = [sb("A_vo%d" % i, [128, 4, 512], BF16) for i in range(2)]
        qr = [sb("A_qr%d" % i, [128, 2, 512], BF16) for i in range(2)]
        t1 = [sb("A_t1%d" % i, [128, 512], F32) for i in range(2)]
        t2 = [sb("A_t2%d" % i, [128, 512], F32) for i in range(2)]
        qTo = [sb("A_qTo%d" % i, [128, 2, 4, 512], BF16) for i in range(2)]
        ident = C.ident_b

        wisrc = C.w_in[l].rearrange("(kt p) n -> p kt n", p=128)
        for cb in (3, 0, 1, 2):
            P.dma("pool", W[:, :, cb * 512:(cb + 1) * 512], wisrc[:, :, cb * 512:(cb + 1) * 512],
                  writes=[("A_w", cb)])
        rsrc = C.c_rope.rearrange("(t p) a d -> p t a d", p=128)
        for i in range(0, NT, 16):
            j = min(NT, i + 16)
            P.dma("sp", rope[:, i:j], rsrc[:, i:j], writes=[("A_rope",)])

        def load_x(mt):
            s = mt % 2
            P.dma("pool", xb[s][:], x_src[mt * 512:(mt + 1) * 512, :].rearrange("(s p) d -> p s d", p=128),
                  writes=[("A_xb", s)])

        load_x(0)
        cnt = dict(tb=0, ub=0, qb=0, qr=0)
        info = {}

        def stage1(g):
            mt, sub = g // 4, g % 4
            s = mt % 2
            if sub == 0:
                if mt + 1 < NMT:
                    load_x(mt + 1)
                for sb_ in range(4):
                    b = cnt["tb"] % 2
                    cnt["tb"] += 1
                    for kt in range(8):
                        P.op("pe", "transpose", out=psb[:, b, kt * 128:(kt + 1) * 128],
                             in_=xb[s][:, sb_, kt * 128:(kt + 1) * 128], identity=ident[:],
                             reads=[("A_xb", s)], writes=[("psT", b)], sig=(kt == 7))
                    P.op("act", "activation", out=xT[s][:, :, sb_ * 128:(sb_ + 1) * 128],
                         in_=psb[:, b, :].rearrange("p (k t) -> p k t", k=8), func=AF.Copy,
                         reads=[("psT", b)], writes=[("A_xT", s, sb_)])
                for m in range(4):
                    b = 2 + cnt["ub"] % 2
                    cnt["ub"] += 1
                    for kt in range(8):
                        P.op("pe", "matmul", out=ps[:, b, :], lhsT=W[:, kt, 1536 + m * 128:1536 + (m + 1) * 128],
                             rhs=xT[s][:, kt, :], start=(kt == 0), stop=(kt == 7),
                             reads=[("A_w", 3)] + [("A_xT", s, i) for i in range(4)], writes=[("ps", b)], sig=(kt == 7))
                    P.op("act", "activation", out=uo[s][:, m, :], in_=ps[:, b, :], func=AF.Copy,
                         reads=[("ps", b)], writes=[("A_uo", s)])
                P.dma("sp", C.uT[:, mt * 512:(mt + 1) * 512].rearrange("(m p) t -> p m t", p=128), uo[s][:],
                      reads=[("A_uo", s)], writes=[("uT", mt)])
            tile = mt * 4 + sub
            banks = []
            for blk in range(3):
                b = 4 + cnt["qb"] % 4
                cnt["qb"] += 1
                banks.append(b)
                for kt in range(8):
                    P.op("pe", "matmul", out=ps[:, b, :], lhsT=xT[s][:, kt, sub * 128:(sub + 1) * 128],
                         rhs=W[:, kt, blk * 512:(blk + 1) * 512], start=(kt == 0), stop=(kt == 7),
                         reads=[("A_w", blk), ("A_xT", s, sub)], writes=[("ps", b)], sig=(kt == 7))
            P.op("act", "activation", out=vo[s][:, sub, :], in_=ps[:, banks[2], :], func=AF.Copy,
                 reads=[("ps", banks[2])], writes=[("A_vo", s)])
            r = cnt["qr"] % 2
            cnt["qr"] += 1
            info[g] = r
            for qk in range(2):
                b = banks[qk]
                src3 = ps[:, b, :].rearrange("p (h d) -> p h d", h=8)
                cc = rope[:, tile, 0:1, :].broadcast_to([128, 8, 64])
                ss = rope[:, tile, 1:2, :].broadcast_to([128, 8, 64])
                t1v = t1[qk][:].rearrange("p (h d) -> p h d", h=8)
                t2v = t2[qk][:].rearrange("p (h d) -> p h d", h=8)
                P.op("dve", "tensor_tensor", out=t1v, in0=src3, in1=cc, op=ALU.mult,
                     reads=[("ps", b), ("A_rope",)], writes=[("A_t1", qk)])
                P.op("dve", "tensor_tensor", out=t2v[:, :, 0:32], in0=src3[:, :, 32:64], in1=ss[:, :, 0:32],
                     op=ALU.mult, reads=[("ps", b), ("A_rope",)], writes=[("A_t2a", qk)])
                P.op("dve", "tensor_tensor", out=t2v[:, :, 32:64], in0=src3[:, :, 0:32], in1=ss[:, :, 32:64],
                     op=ALU.mult, reads=[("ps", b), ("A_rope",)], writes=[("A_t2b", qk)])
                P.op("dve", "tensor_tensor", out=qr[r][:, qk, :], in0=t1[qk][:], in1=t2[qk][:], op=ALU.add,
                     reads=[("A_t1", qk), ("A_t2a", qk), ("A_t2b", qk)], writes=[("A_qr", r, qk)])

        def stage2(g):
            mt, sub = g // 4, g % 4
            s = mt % 2
            r = info.pop(g)
            tb = cnt["tb"] % 2
            cnt["tb"] += 1
            for qk in range(2):
                for h in range(4):
                    P.op("pe", "transpose", out=psb[:, tb, (qk * 4 + h) * 128:(qk * 4 + h + 1) * 128],
                         in_=qr[r][:, qk, h * 128:(h + 1) * 128], identity=ident[:],
                         reads=[("A_qr", r, qk)], writes=[("psT", tb)], sig=(qk == 1 and h == 3))
            P.op("act", "activation", out=qTo[s][:, :, :, sub * 128:(sub + 1) * 128],
                 in_=psb[:, tb, :].rearrange("p (a h t) -> p a h t", a=2, h=4), func=AF.Copy,
                 reads=[("psT", tb)], writes=[("A_qTo", s)])
            if sub == 3:
                P.dma("sp", C.v[mt * 512:(mt + 1) * 512, :].rearrange("(s p) c -> p s c", p=128), vo[s][:],
                      reads=[("A_vo", s)], writes=[("v", mt)])
                P.dma("sp", C.qT[:, :, mt * 512:(mt + 1) * 512].rearrange("h p t -> p h t"), qTo[s][:, 0, :, :],
                      reads=[("A_qTo", s)], writes=[("qT", mt)])
                P.dma("sp", C.kT[:, :, mt * 512:(mt + 1) * 512].rearrange("h p t -> p h t"), qTo[s][:, 1, :, :],
                      reads=[("A_qTo", s)], writes=[("kT", mt)])

        NG_ = NMT * 4
        stage1(0)
        for g in range(NG_):
            if g + 1 < NG_:
                stage1(g + 1)
            stage2(g)
        P.barrier()
        P.flush()


def phase_B(P, nc, C, l):
    L = C.L
    NQT = L // 512
    NKT = L // 128
    lam_init = 0.8 - 0.6 * math.exp(-0.3 * l)
    ps = C.ps
    with ExitStack() as st:
        sb = lambda n, s, d: st.enter_context(nc.sbuf_tensor("%s_l%d" % (n, l), s, d))
        kT = [sb("B_kT%d" % i, [128, L], BF16) for i in range(2)]
        qT = [sb("B_qT%d" % i, [128, L], BF16) for i in range(2)]
        V = [sb("B_V%d" % i, [128, NKT, 128], BF16) for i in range(2)]
        Pt = [sb("B_P%d" % i, [128, 2, 512], BF16) for i in range(3)]
        r1 = sb("B_r1", [128, 512], F32)
        r2 = sb("B_r2", [128, 512], F32)
        o1 = sb("B_o1", [128, 512], F32)
        o2 = sb("B_o2", [128, 512], F32)
        oo = [sb("B_oo%d" % i, [128, 512], F32) for i in range(2)]
        sq = [sb("B_sq%d" % i, [128, 512], BF16) for i in range(2)]
        sd = sb("B_sd", [128, 512], F32)
        rs = sb("B_rs", [128, 512], F32)
        ot = [sb("B_ot%d" % i, [128, 512], BF16) for i in range(2)]
        lq = sb("B_lq", [128, 4, 64], F32)
        lp = sb("B_lp", [128, 2, 64], F32)
        le = sb("B_le", [128, 4], F32)
        gs = sb("B_gs", [128, 2], F32)

        P.dma("sp", lq[:], C.lam_qk[l:l + 1].rearrange("o a d -> o (a d)").partition_broadcast(128)
              .rearrange("p o (a d) -> p (o a) d", a=4), writes=[("B_lq",)])
        P.dma("sp", gs[:, 0:1], C.subln_g[l].rearrange("(p o) -> p o", o=1), writes=[("B_gs0",)])
        lqv = lq[:].rearrange("p (a b) d -> p a b d", b=2)
        P.op("dve", lambda e: e.tensor_tensor(out=lp[:], in0=lqv[:, :, 0, :], in1=lqv[:, :, 1, :], op=ALU.mult),
             reads=[("B_lq",)], writes=[("B_lp",)])
        P.op("dve", lambda e: e.reduce_sum(out=le[:, 0:2], in_=lp[:], axis=AX.X),
             reads=[("B_lp",)], writes=[("B_le0",)])
        P.op("act", lambda e: e.activation(out=le[:, 2:4], in_=le[:, 0:2], func=AF.Exp),
             reads=[("B_le0",)], writes=[("B_le1",)])
        P.op("dve", lambda e: e.tensor_tensor(out=le[:, 0:1], in0=le[:, 3:4], in1=le[:, 2:3], op=ALU.subtract),
             reads=[("B_le1",)], writes=[("B_le2",)])
        P.op("dve", lambda e: e.tensor_scalar(out=le[:, 1:2], in0=le[:, 0:1], scalar1=-lam_init, scalar2=None,
                                              op0=ALU.add),
             reads=[("B_le2",)], writes=[("B_neglam",)])
        P.op("dve", lambda e: e.tensor_scalar(out=gs[:, 1:2], in0=gs[:, 0:1], scalar1=(1.0 - lam_init), scalar2=None,
                                              op0=ALU.mult),
             reads=[("B_gs0",)], writes=[("B_gs",)])
        neglam = le[:, 1:2]
        gsc = gs[:, 1:2]

        def load_head(h):
            s = h % 2
            P.dma("sp", kT[s][:], C.kT[h], reads=[("kT", i) for i in range(NQT)], writes=[("B_kT", s)])
            P.dma("sp", qT[s][:], C.qT[h], reads=[("qT", i) for i in range(NQT)], writes=[("B_qT", s)])
            vsrc = C.v[:, h * 128:(h + 1) * 128].rearrange("(kt p) c -> p kt c", p=128)
            step = max(1, NKT // 4)
            for i in range(0, NKT, step):
                P.dma("sp", V[s][:, i:i + step, :], vsrc[:, i:i + step, :],
                      reads=[("v", j) for j in range(NQT)], writes=[("B_V", s)])

        load_head(0)
        it = 0
        ep = 0
        pending = []
        tri3 = C.tri_b[:].unsqueeze(1).broadcast_to([128, 2, 128])

        def emit_S(s, qt, kt, it_):
            b0 = 2 * (it_ % 2)
            off = max(0, kt - 4 * qt) * 128
            for m in range(2):
                P.op("pe", "matmul", out=ps[:, b0 + m, off:512],
                     lhsT=kT[s][m * 64:(m + 1) * 64, kt * 128:(kt + 1) * 128],
                     rhs=qT[s][m * 64:(m + 1) * 64, qt * 512 + off:(qt + 1) * 512], start=True, stop=True,
                     reads=[("B_kT", s), ("B_qT", s)], writes=[("ps", b0 + m)], sig=(m == 1))
            r = it_ % 3
            P.op("act", "activation", out=Pt[r][:, :, off:512], in_=ps[:, b0:b0 + 2, off:512], func=AF.Exp,
                 scale=0.125, reads=[("ps", b0), ("ps", b0 + 1)], writes=[("B_P", r)])
            if kt >= 4 * qt:
                P.op("pool", "tensor_tensor", out=Pt[r][:, :, off:off + 128], in0=Pt[r][:, :, off:off + 128],
                     in1=tri3, op=ALU.mult, reads=[("B_P", r), ("tri_b",)], writes=[("B_P", r)])

        def emit_PV(s, qt, kt, it_, nk):
            r = it_ % 3
            off = max(0, kt - 4 * qt) * 128
            last = (kt == nk - 1)
            for m in range(2):
                P.op("pe", "matmul", out=ps[:, 4 + m, off:512], lhsT=V[s][:, kt, :], rhs=Pt[r][:, m, off:512],
                     start=(kt == 0), stop=last,
                     reads=[("B_V", s), ("B_P", r)], writes=[("ps", 4 + m)], sig=False)
            for m in range(2):
                P.op("pe", "matmul", out=ps[:, 6 + m, off:512], lhsT=C.ones_b[:], rhs=Pt[r][:, m, off:512],
                     start=(kt == 0), stop=last,
                     reads=[("ones_b",), ("B_P", r)], writes=[("ps", 6 + m)], sig=(m == 1))

        def make_tail(e_, h, qt):
            def tail(bq):
                P.op("pe", "matmul", out=ps[:, bq, :], lhsT=C.ones_b[:], rhs=sq[e_][:], start=True, stop=True,
                     reads=[("ones_b",), ("B_sq", e_)], writes=[("ps", bq)])
                P.op("act", "activation", out=sd[:], in_=ps[:, bq, :], func=AF.Sqrt, scale=1.0 / 128.0,
                     bias=C.eps_col[:, 0:1], reads=[("ps", bq), ("eps_col",)], writes=[("B_sd",)])
                P.op("dve", "reciprocal", out=rs[:], in_=sd[:], reads=[("B_sd",)], writes=[("B_rs",)])
                P.op("dve", "scalar_tensor_tensor", out=ot[e_][:], in0=oo[e_][:], scalar=gsc, in1=rs[:],
                     op0=ALU.mult, op1=ALU.mult,
                     reads=[("B_oo", e_), ("B_rs",), ("B_gs",)], writes=[("B_ot", e_)])
                P.dma("sp", C.mixT[h * 128:(h + 1) * 128, qt * 512:(qt + 1) * 512], ot[e_][:],
                      reads=[("B_ot", e_)], writes=[("mixT", h, qt)])
            return tail

        for h in range(N_HEADS):
            s = h % 2
            if h + 1 < N_HEADS:
                load_head(h + 1)
            for qt in range(NQT):
                nk = 4 * (qt + 1)
                emit_S(s, qt, 0, it)
                for kt in range(nk):
                    if kt + 1 < nk:
                        emit_S(s, qt, kt + 1, it + kt + 1)
                    emit_PV(s, qt, kt, it + kt, nk)
                    if kt == 1 and pending:
                        for f in pending:
                            f(2 * ((it + kt + 2) % 2))
                        pending = []
                it += nk
                e_ = ep % 2
                ep += 1
                P.op("dve", "reciprocal", out=r1[:], in_=ps[:, 6, :], reads=[("ps", 6)], writes=[("B_r1",)])
                P.op("dve", "tensor_tensor", out=o1[:], in0=ps[:, 4, :], in1=r1[:], op=ALU.mult,
                     reads=[("ps", 4), ("B_r1",)], writes=[("B_o1",)])
                P.op("dve", "reciprocal", out=r2[:], in_=ps[:, 7, :], reads=[("ps", 7)], writes=[("B_r2",)])
                P.op("dve", "tensor_tensor", out=o2[:], in0=ps[:, 5, :], in1=r2[:], op=ALU.mult,
                     reads=[("ps", 5), ("B_r2",)], writes=[("B_o2",)])
                P.op("dve", "scalar_tensor_tensor", out=oo[e_][:], in0=o2[:], scalar=neglam, in1=o1[:],
                     op0=ALU.mult, op1=ALU.add,
                     reads=[("B_o1",), ("B_o2",), ("B_neglam",)], writes=[("B_oo", e_)])
                P.op("act", "activation", out=sq[e_][:], in_=oo[e_][:], func=AF.Square,
                     reads=[("B_oo", e_)], writes=[("B_sq", e_)])
                pending.append(make_tail(e_, h, qt))
        for f in pending:
            f(0)
        pending = []
        for f in pending:
            f()
        P.barrier()
        P.flush()


R1 = 16


def cmul_acc(P, dst_re, dst_im, src_re, src_im, wr, wi, wni, kd_re, kd_im, ks_re, ks_im, extra_reads=()):
    er = list(extra_reads)
    P.op("dve", "scalar_tensor_tensor", out=dst_re, in0=src_re, scalar=wr, in1=dst_re, op0=ALU.mult, op1=ALU.add,
         reads=[ks_re, kd_re] + er, writes=[kd_re])
    P.op("dve", "scalar_tensor_tensor", out=dst_im, in0=src_im, scalar=wr, in1=dst_im, op0=ALU.mult, op1=ALU.add,
         reads=[ks_im, kd_im] + er, writes=[kd_im])
    P.op("dve", "scalar_tensor_tensor", out=dst_re, in0=src_im, scalar=wni, in1=dst_re, op0=ALU.mult, op1=ALU.add,
         reads=[ks_im, kd_re] + er, writes=[kd_re])
    P.op("dve", "scalar_tensor_tensor", out=dst_im, in0=src_re, scalar=wi, in1=dst_im, op0=ALU.mult, op1=ALU.add,
         reads=[ks_re, kd_im] + er, writes=[kd_im])


def ssm_prep(P, nc, C, l, st, NS):
    ps = C.ps
    sb = lambda n, s, d: st.enter_context(nc.sbuf_tensor("%s_l%d" % (n, l), s, d))
    BT = sb("C_BT", [128, 32, 128], BF16)
    CT = sb("C_CT", [128, 32, 128], BF16)
    NPW = R1 + NS + 1
    PWR = sb("C_pwr", [128, NPW, 16], F32)
    PWI = sb("C_pwi", [128, NPW, 16], F32)
    PWN = sb("C_pwn", [128, NPW, 16], F32)
    Dcol = sb("C_D", [128, 4], F32)
    with ExitStack() as st2:
        sb2 = lambda n, s, d: st2.enter_context(nc.sbuf_tensor("%s_l%d" % (n, l), s, d))
        lre = sb2("C_lre", [128, 16], F32)
        lim = sb2("C_lim", [128, 16], F32)
        ldt = sb2("C_ldt", [128, 16], F32)
        tmp = [sb2("C_tmp%d" % i, [128, 16], F32) for i in range(12)]
        bre = sb2("C_bre", [128, 16, 16], F32)
        bim = sb2("C_bim", [128, 16, 16], F32)
        Bre = sb2("C_Bre", [128, 16, 16], F32)
        Bim = sb2("C_Bim", [128, 16, 16], F32)
        tb1 = sb2("C_tb1", [128, 16, 16], F32)
        WB = sb2("C_WB", [128, 16, 2, 128], F32)
        Cn = sb2("C_Cn", [16, 2, 2048], F32)
        halfpi = sb2("C_hpi", [128, 1], F32)
        slow = dict(allow_slow_non_contiguous=True)
        P.dma("sp", lre[:], C.ssm_lam_re[l].rearrange("(k g) p -> (g p) k", g=2), writes=[("C_lre",)], **slow)
        P.dma("sp", lim[:], C.ssm_lam_im[l].rearrange("(k g) p -> (g p) k", g=2), writes=[("C_lim",)], **slow)
        ldsrc = C.ssm_log_dt[l].rearrange("(k g) -> g k", g=2)
        for g2 in range(2):
            P.dma("sp", ldt[g2 * 64:(g2 + 1) * 64, :].unsqueeze(1), ldsrc[g2:g2 + 1, :].partition_broadcast(64),
                  writes=[("C_ldt", g2)], **slow)
        P.dma("sp", bre[:], C.ssm_b_re[l].rearrange("(k g) p c -> (g p) k c", g=2), writes=[("C_bre",)])
        P.dma("sp", bim[:], C.ssm_b_im[l].rearrange("(k g) p c -> (g p) k c", g=2), writes=[("C_bim",)])
        P.dma("sp", Cn[:, 0, :].rearrange("c (g p) -> c g p", p=64), C.ssm_c_re[l].rearrange("g c p -> c g p"),
              writes=[("C_Cn", 0)])
        P.dma("sp", Cn[:, 1, :].rearrange("c (g p) -> c g p", p=64), C.ssm_c_im[l].rearrange("g c p -> c g p"),
              writes=[("C_Cn", 1)])
        P.dma("sp", Dcol[:], C.ssm_d[l].rearrange("g c -> (g c)").rearrange("(m p) -> p m", p=128),
              writes=[("C_D",)], **slow)
        P.op("pool", "memset", ap=halfpi[:], constant=math.pi / 2, writes=[("C_hpi",)])
        P.op("pool", "memset", ap=WB[:], constant=0.0, writes=[("C_WB",)])
        P.op("pool", "memset", ap=CT[:], constant=0.0, writes=[("C_CT",)])

        cnt = [0]

        def V(name, out, in0, in1, op, rd, wr):
            P.op("dve", "tensor_tensor", out=out, in0=in0, in1=in1, op=op, reads=rd, writes=wr)

        def tk(i):
            return ("C_tmp", i)

        dt_, lrd, th, mag, cs_, sn_, ta, tb_, tc, are, aim, den = tmp
        P.op("act", "activation", out=dt_[:], in_=ldt[:], func=AF.Exp,
             reads=[("C_ldt", 0), ("C_ldt", 1)], writes=[tk(0)])
        V("lrd", lrd[:], lre[:], dt_[:], ALU.mult, [("C_lre",), tk(0)], [tk(1)])
        V("th", th[:], lim[:], dt_[:], ALU.mult, [("C_lim",), tk(0)], [tk(2)])
        P.op("act", "activation", out=mag[:], in_=lrd[:], func=AF.Exp, reads=[tk(1)], writes=[tk(3)])
        P.op("act", "activation", out=sn_[:], in_=th[:], func=AF.Sin, scale=0.125, reads=[tk(2)], writes=[tk(5)])
        P.op("act", "activation", out=cs_[:], in_=th[:], func=AF.Sin, scale=-0.125, bias=halfpi[:, 0:1],
             reads=[tk(2), ("C_hpi",)], writes=[tk(4)])
        for _ in range(3):
            V("", ta[:], cs_[:], cs_[:], ALU.mult, [tk(4)], [tk(6)])
            V("", tb_[:], sn_[:], sn_[:], ALU.mult, [tk(5)], [tk(7)])
            V("", tc[:], cs_[:], sn_[:], ALU.mult, [tk(4), tk(5)], [tk(8)])
            V("", cs_[:], ta[:], tb_[:], ALU.subtract, [tk(6), tk(7)], [tk(4)])
            V("", sn_[:], tc[:], tc[:], ALU.add, [tk(8)], [tk(5)])
        V("", are[:], mag[:], cs_[:], ALU.mult, [tk(3), tk(4)], [tk(9)])
        V("", aim[:], mag[:], sn_[:], ALU.mult, [tk(3), tk(5)], [tk(10)])
        kp = lambda i: ("C_pw", i)

        def pw_set(i, re_ap, im_ap, rd):
            P.op("dve", "tensor_copy", out=PWR[:, i, :], in_=re_ap, reads=rd, writes=[("C_pwr", i)])
            P.op("dve", "tensor_copy", out=PWI[:, i, :], in_=im_ap, reads=rd, writes=[("C_pwi", i)])
            P.op("dve", "tensor_scalar", out=PWN[:, i, :], in0=im_ap, scalar1=-1.0, scalar2=None, op0=ALU.mult,
                 reads=rd, writes=[("C_pwn", i)])

        def pw_mul(i, j, b):
            rd = [("C_pwr", j), ("C_pwi", j), ("C_pwr", b), ("C_pwi", b), ("C_pwn", b)]
            V("", ta[:], PWR[:, j, :], PWR[:, b, :], ALU.mult, rd, [tk(6)])
            V("", tb_[:], PWI[:, j, :], PWN[:, b, :], ALU.mult, rd, [tk(7)])
            V("", PWR[:, i, :], ta[:], tb_[:], ALU.add, [tk(6), tk(7)], [("C_pwr", i)])
            V("", ta[:], PWR[:, j, :], PWI[:, b, :], ALU.mult, rd, [tk(6)])
            V("", tb_[:], PWI[:, j, :], PWR[:, b, :], ALU.mult, rd, [tk(7)])
            V("", PWI[:, i, :], ta[:], tb_[:], ALU.add, [tk(6), tk(7)], [("C_pwi", i)])
            P.op("dve", "tensor_scalar", out=PWN[:, i, :], in0=PWI[:, i, :], scalar1=-1.0, scalar2=None,
                 op0=ALU.mult, reads=[("C_pwi", i)], writes=[("C_pwn", i)])

        pw_set(1, are[:], aim[:], [tk(9), tk(10)])
        for n in range(2, R1 + 1):
            pw_mul(n, n - 1, 1)
        for s_ in range(1, NS):
            pw_mul(R1 + s_, R1 + s_ - 1, R1 + s_ - 1)
        nr, fr, fi = ta, tb_, tc
        P.op("dve", "tensor_scalar", out=nr[:], in0=are[:], scalar1=-1.0, scalar2=None, op0=ALU.add,
             reads=[tk(9)], writes=[tk(6)])
        V("", den[:], lre[:], lre[:], ALU.mult, [("C_lre",)], [tk(11)])
        V("", dt_[:], lim[:], lim[:], ALU.mult, [("C_lim",)], [tk(0)])
        V("", den[:], den[:], dt_[:], ALU.add, [tk(11), tk(0)], [tk(11)])
        P.op("dve", "reciprocal", out=den[:], in_=den[:], reads=[tk(11)], writes=[tk(11)])
        V("", fr[:], nr[:], lre[:], ALU.mult, [tk(6), ("C_lre",)], [tk(7)])
        V("", dt_[:], aim[:], lim[:], ALU.mult, [tk(10), ("C_lim",)], [tk(0)])
        V("", fr[:], fr[:], dt_[:], ALU.add, [tk(7), tk(0)], [tk(7)])
        V("", fr[:], fr[:], den[:], ALU.mult, [tk(7), tk(11)], [tk(7)])
        V("", fi[:], aim[:], lre[:], ALU.mult, [tk(10), ("C_lre",)], [tk(8)])
        V("", dt_[:], nr[:], lim[:], ALU.mult, [tk(6), ("C_lim",)], [tk(0)])
        V("", fi[:], fi[:], dt_[:], ALU.subtract, [tk(8), tk(0)], [tk(8)])
        V("", fi[:], fi[:], den[:], ALU.mult, [tk(8), tk(11)], [tk(8)])
        frb = fr[:].unsqueeze(2).broadcast_to([128, 16, 16])
        fib = fi[:].unsqueeze(2).broadcast_to([128, 16, 16])
        V("", Bre[:], bre[:], frb, ALU.mult, [("C_bre",), tk(7)], [("C_Bre",)])
        V("", tb1[:], bim[:], fib, ALU.mult, [("C_bim",), tk(8)], [("C_tb1",)])
        V("", Bre[:], Bre[:], tb1[:], ALU.subtract, [("C_Bre",), ("C_tb1",)], [("C_Bre",)])
        V("", Bim[:], bim[:], frb, ALU.mult, [("C_bim",), tk(7)], [("C_Bim",)])
        V("", tb1[:], bre[:], fib, ALU.mult, [("C_bre",), tk(8)], [("C_tb1",)])
        V("", Bim[:], Bim[:], tb1[:], ALU.add, [("C_Bim",), ("C_tb1",)], [("C_Bim",)])
        for g2 in range(2):
            for q4 in range(4):
                c0 = q4 * 32 + g2 * 16
                for ri, src in ((0, Bre), (1, Bim)):
                    P.op("dve", "tensor_copy", out=WB[g2 * 64:(g2 + 1) * 64, q4::4, ri, c0:c0 + 16],
                         in_=src[g2 * 64:(g2 + 1) * 64, q4::4, :],
                         reads=[("C_Bre",), ("C_Bim",), ("C_WB",)], writes=[("C_WBf", g2, q4, ri)])
        wbf = [("C_WBf", g2, q4, ri) for g2 in range(2) for q4 in range(4) for ri in range(2)]
        for grp in range(8):
            b = grp % 2
            for j in range(4):
                idx = grp * 4 + j
                P.op("pe", "transpose", out=ps[:, b, j * 128:(j + 1) * 128], in_=WB[:, idx // 2, idx % 2, :],
                     identity=C.ident_f[:], reads=wbf + [("ident_f",)], writes=[("ps", b)], sig=(j == 3))
            P.op("act", "activation", out=BT[:, grp * 4:(grp + 1) * 4, :],
                 in_=ps[:, b, :].rearrange("p (j c) -> p j c", j=4), func=AF.Copy,
                 reads=[("ps", b)], writes=[("C_BT", grp)])
        for k in range(16):
            for ri in range(2):
                P.op("pe", "transpose", out=ps[:, 2, (k * 2 + ri) * 16:(k * 2 + ri + 1) * 16],
                     in_=Cn[0:16, ri, k * 128:(k + 1) * 128], identity=C.ident_f[0:16, 0:16],
                     reads=[("C_Cn", ri), ("ident_f",)], writes=[("ps", 2)], sig=(k == 15 and ri == 1))
        psC = ps[:, 2, :].rearrange("p (k r c) -> p k r c", k=16, r=2)
        CTv = CT[:].rearrange("p (k r) c -> p k r c", r=2)
        for g2 in range(2):
            for q4 in range(4):
                c0 = q4 * 32 + g2 * 16
                for ri in range(2):
                    P.op("act", "activation", out=CTv[g2 * 64:(g2 + 1) * 64, q4::4, ri, c0:c0 + 16],
                         in_=psC[g2 * 64:(g2 + 1) * 64, q4::4, ri, :], func=AF.Copy,
                         scale=(1.0 if ri == 0 else -1.0),
                         reads=[("ps", 2), ("C_CT",)], writes=[("C_CTf", g2, q4, ri)])
        P.barrier()
        P.flush()
    ctf = [("C_CTf", g2, q4, ri) for g2 in range(2) for q4 in range(4) for ri in range(2)]

    return BT, CT, PWR, PWI, PWN, Dcol, ctf


def phase_C(P, nc, C, l):
    L = C.L
    TS = L
    J1 = TS // R1
    NS = J1.bit_length() - 1
    assert (1 << NS) == J1
    NB = TS // 512
    ps = C.ps
    with ExitStack() as st:
        sb = lambda n, s, d: st.enter_context(nc.sbuf_tensor("%s_l%d" % (n, l), s, d))
        BT, CT, PWR, PWI, PWN, Dcol, ctf = ssm_prep(P, nc, C, l, st, NS)
        assert J1 <= 512
        X = sb("C_X", [128, 2, R1, J1], F32)
        Xb = sb("C_Xb", [128, 2, R1, J1], BF16)
        uTt = sb("C_uTt", [128, TS], BF16)
        yraw = sb("C_yraw", [128, R1, J1], F32)
        ta_ = [sb("C_ga%d" % i, [128, 512], F32) for i in range(2)]
        tb_2 = [sb("C_gb%d" % i, [128, 512], F32) for i in range(2)]
        go = [sb("C_go%d" % i, [128, 512], BF16) for i in range(2)]
        HS = [sb("C_hs%d" % i, [128, 2, J1], F32) for i in range(2)]
        bu_rr = 0
        y_rr = 0
        g_rr = 0
        NQ = L // 512
        RG = 4
        kx = lambda c, r: ("C_Xc", c, r)
        for m in range(4):
            P.dma("sp", uTt[:], C.uT[m * 128:(m + 1) * 128, :], reads=[("uT", i) for i in range(NQ)],
                  writes=[("C_uTt",)])
            for q4 in range(4):
                k = m * 4 + q4
                sc = lambda T_, i, k=k: T_[:, i, k:k + 1]
                for r in range(R1):
                    b0 = 2 * (bu_rr % 2)
                    bu_rr += 1
                    for ri in range(2):
                        P.op("pe", "matmul", out=ps[:, b0 + ri, 0:J1], lhsT=BT[:, k * 2 + ri, :],
                             rhs=uTt[:, r::R1], start=True, stop=True,
                             reads=[("C_BT", (k * 2 + ri) // 4), ("C_uTt",)], writes=[("ps", b0 + ri)], sig=(ri == 1))
                    P.op("act", "activation", out=X[:, :, r, :], in_=ps[:, b0:b0 + 2, 0:J1],
                         func=AF.Copy, reads=[("ps", b0), ("ps", b0 + 1)], writes=[kx(0, r), kx(1, r)])
                for r in range(1, R1):
                    cmul_acc(P, X[:, 0, r, :], X[:, 1, r, :], X[:, 0, r - 1, :], X[:, 1, r - 1, :],
                             sc(PWR, 1), sc(PWI, 1), sc(PWN, 1), kx(0, r), kx(1, r), kx(0, r - 1), kx(1, r - 1))
                klast = (kx(0, R1 - 1), kx(1, R1 - 1))
                src = (X[:, 0, R1 - 1, :], X[:, 1, R1 - 1, :])
                srck = klast
                Xtop = X[:, :, R1 - 1, :]
                src2 = Xtop
                for s_ in range(NS):
                    d = 1 << s_
                    if s_ == NS - 1:
                        dst = (X[:, 0, R1 - 1, :], X[:, 1, R1 - 1, :])
                        dstk = klast
                        dst2 = Xtop
                    else:
                        hb = HS[s_ % 2]
                        dst = (hb[:, 0, :], hb[:, 1, :])
                        dstk = (("C_hs", s_ % 2, 0), ("C_hs", s_ % 2, 1))
                        dst2 = hb[:]
                    wr, wi, wni = sc(PWR, R1 + s_), sc(PWI, R1 + s_), sc(PWN, R1 + s_)
                    if dst2 is not src2:
                        P.op("dve", "tensor_copy", out=dst2[:, :, 0:d], in_=src2[:, :, 0:d],
                             reads=list(srck), writes=list(dstk))
                    n = J1 - d
                    P.op("dve", "scalar_tensor_tensor", out=dst[0][:, d:J1], in0=src[0][:, 0:n], scalar=wr,
                         in1=src[0][:, d:J1], op0=ALU.mult, op1=ALU.add, reads=list(srck), writes=[dstk[0]])
                    P.op("dve", "scalar_tensor_tensor", out=dst[1][:, d:J1], in0=src[1][:, 0:n], scalar=wr,
                         in1=src[1][:, d:J1], op0=ALU.mult, op1=ALU.add, reads=list(srck), writes=[dstk[1]])
                    P.op("dve", "scalar_tensor_tensor", out=dst[0][:, d:J1], in0=src[1][:, 0:n], scalar=wni,
                         in1=dst[0][:, d:J1], op0=ALU.mult, op1=ALU.add, reads=list(srck) + [dstk[0]],
                         writes=[dstk[0]])
                    P.op("dve", "scalar_tensor_tensor", out=dst[1][:, d:J1], in0=src[0][:, 0:n], scalar=wi,
                         in1=dst[1][:, d:J1], op0=ALU.mult, op1=ALU.add, reads=list(srck) + [dstk[1]],
                         writes=[dstk[1]])
                    src, srck, src2 = dst, dstk, dst2
                for r in range(R1 - 1):
                    cmul_acc(P, X[:, 0, r, 1:J1], X[:, 1, r, 1:J1],
                             X[:, 0, R1 - 1, 0:J1 - 1], X[:, 1, R1 - 1, 0:J1 - 1],
                             sc(PWR, r + 1), sc(PWI, r + 1), sc(PWN, r + 1),
                             kx(0, r), kx(1, r), klast[0], klast[1])
                for rg in range(R1 // RG):
                    rs_ = slice(rg * RG, (rg + 1) * RG)
                    P.op("act", "activation", out=Xb[:, :, rs_, :], in_=X[:, :, rs_, :], func=AF.Copy,
                         reads=[kx(c, r) for c in range(2) for r in range(rg * RG, (rg + 1) * RG)] + list(klast),
                         writes=[("C_Xb", rg)])
                for r in range(R1):
                    b = 4 + y_rr % 4
                    y_rr += 1
                    for ri in range(2):
                        P.op("pe", "matmul", out=ps[:, b, 0:J1], lhsT=CT[:, k * 2 + ri, :],
                             rhs=Xb[:, ri, r, :], start=(ri == 0), stop=(ri == 1),
                             reads=ctf + [("C_Xb", r // RG)], writes=[("ps", b)], sig=(ri == 1))
                    P.op("act", "activation", out=yraw[q4 * 32:(q4 + 1) * 32, r, :],
                         in_=ps[q4 * 32:(q4 + 1) * 32, b, 0:J1], func=AF.Copy,
                         reads=[("ps", b)], writes=[("C_yraw", q4, r)])
            for r in range(R1):
                i = g_rr % 2
                g_rr += 1
                P.op("dve", "scalar_tensor_tensor", out=ta_[i][:, 0:J1], in0=uTt[:, r::R1], scalar=Dcol[:, m:m + 1],
                     in1=yraw[:, r, :], op0=ALU.mult, op1=ALU.add,
                     reads=[("C_uTt",), ("C_D",)] + [("C_yraw", q, r) for q in range(4)], writes=[("C_ga", i)])
                P.op("pool", "tensor_tensor", out=tb_2[i][:, 0:J1], in0=ta_[i][:, 0:J1], in1=ta_[i][:, 0:J1],
                     op=ALU.mult, reads=[("C_ga", i)], writes=[("C_gb", i)])
                P.op("pool", "tensor_scalar", out=tb_2[i][:, 0:J1], in0=tb_2[i][:, 0:J1], scalar1=0.044715,
                     scalar2=1.0, op0=ALU.mult, op1=ALU.add, reads=[("C_gb", i)], writes=[("C_gb", i)])
                P.op("pool", "tensor_tensor", out=tb_2[i][:, 0:J1], in0=tb_2[i][:, 0:J1], in1=ta_[i][:, 0:J1],
                     op=ALU.mult, reads=[("C_gb", i), ("C_ga", i)], writes=[("C_gb", i)])
                P.op("act", "activation", out=tb_2[i][:, 0:J1], in_=tb_2[i][:, 0:J1], func=AF.Sigmoid,
                     scale=2.0 * math.sqrt(2.0 / math.pi), reads=[("C_gb", i)], writes=[("C_gb", i)])
                P.op("pool", "tensor_tensor", out=go[i][:, 0:J1], in0=ta_[i][:, 0:J1], in1=tb_2[i][:, 0:J1],
                     op=ALU.mult, reads=[("C_ga", i), ("C_gb", i)], writes=[("C_go", i)])
                P.dma("sp", C.gT[m * 128:(m + 1) * 128, r * J1:(r + 1) * J1], go[i][:, 0:J1],
                      reads=[("C_go", i)], writes=[("gT", m, r)])
        P.barrier()
        P.flush()
    with ExitStack() as st:
        sb = lambda n, s, d: st.enter_context(nc.sbuf_tensor("%s_l%d" % (n, l), s, d))
        GW = sb("C_GW", [128, 4, 512], BF16)
        gb = sb("C_gbias", [128, 4], F32)
        gt = [sb("C_gt%d" % i, [128, 4, J1], BF16) for i in range(2)]
        sg = [sb("C_sg%d" % i, [128, 512], F32) for i in range(2)]
        SO = sb("C_SO", [128, 4, L], BF16)
        P.dma("pool", GW[:], C.glu_w[l].rearrange("(kt p) n -> p kt n", p=128), writes=[("C_GW",)])
        P.dma("sp", gb[:], C.glu_b[l].rearrange("(m p) -> p m", p=128), writes=[("C_gbias",)],
              allow_slow_non_contiguous=True)
        rr = 0
        for r in range(R1):
            i = r % 2
            P.dma("sp", gt[i][:], C.gT[:, r * J1:(r + 1) * J1].rearrange("(m p) t -> p m t", p=128),
                  reads=[("gT", m, r) for m in range(4)], writes=[("C_gt", i)])
            for mo in range(4):
                b = rr % 4
                rr += 1
                for kt in range(4):
                    P.op("pe", "matmul", out=ps[:, b, 0:J1], lhsT=GW[:, kt, mo * 128:(mo + 1) * 128],
                         rhs=gt[i][:, kt, :], start=(kt == 0), stop=(kt == 3),
                         reads=[("C_GW",), ("C_gt", i)], writes=[("ps", b)], sig=(kt == 3))
                j = rr % 2
                P.op("act", "activation", out=sg[j][:, 0:J1], in_=ps[:, b, 0:J1], func=AF.Sigmoid,
                     bias=gb[:, mo:mo + 1], reads=[("ps", b), ("C_gbias",)], writes=[("C_sg", j)])
                P.op("dve", "tensor_tensor", out=SO[:, mo, r::R1], in0=gt[i][:, mo, :], in1=sg[j][:, 0:J1],
                     op=ALU.mult, reads=[("C_gt", i), ("C_sg", j)], writes=[("C_SO", mo, r)])
        for mo in range(4):
            for hf in range(2):
                sl = slice(hf * (L // 2), (hf + 1) * (L // 2))
                P.dma("sp", C.mixT[512 + mo * 128:512 + (mo + 1) * 128, sl], SO[:, mo, sl],
                      reads=[("C_SO", mo, r) for r in range(R1)], writes=[("mixT", 4, mo, hf)])
        P.barrier()
        P.flush()


class Rot:
    def __init__(self):
        self.i = 0

    def next(self):
        b = 2 * (self.i % 2)
        self.i += 1
        return b


def phase_BC(P, nc, C, l):
    from collections import deque
    L = C.L
    NQT = L // 512
    NKT = L // 128
    import os
    TS = int(os.environ.get("BC_TS", min(L, 4096)))
    NH = L // TS
    J1 = TS // R1
    NS = J1.bit_length() - 1
    assert (1 << NS) == J1 and J1 <= 512
    RG = 4
    lam_init = 0.8 - 0.6 * math.exp(-0.3 * l)
    ps = C.ps
    with ExitStack() as st:
        sb = lambda n, s, d: st.enter_context(nc.sbuf_tensor("%s_l%d" % (n, l), s, d))
        BT, CT, PWR, PWI, PWN, Dcol, ctf = ssm_prep(P, nc, C, l, st, NS)
        kT = sb("B_kT", [128, L], BF16)
        V = sb("B_V", [128, NKT, 128], BF16)
        qTt = [sb("B_qT%d" % i, [128, 512], BF16) for i in range(2)]
        Pt = [sb("B_P%d" % i, [128, 2, 512], BF16) for i in range(3)]
        r1 = sb("B_r1", [128, 512], F32)
        r2 = sb("B_r2", [128, 512], F32)
        o1 = sb("B_o1", [128, 512], F32)
        o2 = sb("B_o2", [128, 512], F32)
        oo = [sb("B_oo%d" % i, [128, 512], F32) for i in range(2)]
        sq = [sb("B_sq%d" % i, [128, 512], BF16) for i in range(2)]
        sd = sb("B_sd", [128, 512], F32)
        rs = sb("B_rs", [128, 512], F32)
        ot = [sb("B_ot%d" % i, [128, 512], BF16) for i in range(2)]
        lq = sb("B_lq", [128, 4, 64], F32)
        lp = sb("B_lp", [128, 2, 64], F32)
        le = sb("B_le", [128, 4], F32)
        gs = sb("B_gs", [128, 2], F32)
        X = [sb("C_X%d" % i, [128, 2, R1, J1], F32) for i in range(2)]
        Xb = [sb("C_Xb%d" % i, [128, 2, RG, J1], BF16) for i in range(2)]
        uTt = [sb("C_uTt%d" % i, [128, TS], BF16) for i in range(2)]
        yraw = sb("C_yraw", [128, R1, J1], F32)
        carry = sb("C_carry", [128, 16, 2], F32)
        ta_ = [sb("C_ga%d" % i, [128, J1], F32) for i in range(2)]
        tb_2 = [sb("C_gb%d" % i, [128, J1], F32) for i in range(2)]
        go = [sb("C_go%d" % i, [128, J1], BF16) for i in range(2)]
        HS = [sb("C_hs%d" % i, [128, 2, J1], F32) for i in range(2)]

        P.dma("sp", lq[:], C.lam_qk[l:l + 1].rearrange("o a d -> o (a d)").partition_broadcast(128)
              .rearrange("p o (a d) -> p (o a) d", a=4), writes=[("B_lq",)])
        P.dma("sp", gs[:, 0:1], C.subln_g[l].rearrange("(p o) -> p o", o=1), writes=[("B_gs0",)])
        lqv = lq[:].rearrange("p (a b) d -> p a b d", b=2)
        P.op("dve", "tensor_tensor", out=lp[:], in0=lqv[:, :, 0, :], in1=lqv[:, :, 1, :], op=ALU.mult,
             reads=[("B_lq",)], writes=[("B_lp",)])
        P.op("dve", "reduce_sum", out=le[:, 0:2], in_=lp[:], axis=AX.X, reads=[("B_lp",)], writes=[("B_le0",)])
        P.op("act", "activation", out=le[:, 2:4], in_=le[:, 0:2], func=AF.Exp, reads=[("B_le0",)], writes=[("B_le1",)])
        P.op("dve", "tensor_tensor", out=le[:, 0:1], in0=le[:, 3:4], in1=le[:, 2:3], op=ALU.subtract,
             reads=[("B_le1",)], writes=[("B_le2",)])
        P.op("dve", "tensor_scalar", out=le[:, 1:2], in0=le[:, 0:1], scalar1=-lam_init, scalar2=None, op0=ALU.add,
             reads=[("B_le2",)], writes=[("B_neglam",)])
        P.op("dve", "tensor_scalar", out=gs[:, 1:2], in0=gs[:, 0:1], scalar1=(1.0 - lam_init), scalar2=None,
             op0=ALU.mult, reads=[("B_gs0",)], writes=[("B_gs",)])
        neglam = le[:, 1:2]
        gsc = gs[:, 1:2]
        tri3 = C.tri_b[:].unsqueeze(1).broadcast_to([128, 2, 128])
        rot = Rot()

        pslots = deque()
        bst = dict(pt=0, ep=0)
        pending = []

        NCH = min(8, NKT)
        KPC = NKT // NCH

        def load_head(h):
            vsrc = C.v[:, h * 128:(h + 1) * 128].rearrange("(kt p) c -> p kt c", p=128)
            for c_ in range(NCH):
                k0, k1 = c_ * KPC, (c_ + 1) * KPC
                P.dma("sp", kT[:, k0 * 128:k1 * 128], C.kT[h, :, k0 * 128:k1 * 128], writes=[("B_kT", c_)])
                P.dma("sp", V[:, k0:k1, :], vsrc[:, k0:k1, :], writes=[("B_V", c_)])

        def load_q(h, qt):
            s2 = (h * NQT + qt) % 2
            P.dma("sp", qTt[s2][:], C.qT[h, :, qt * 512:(qt + 1) * 512], writes=[("B_qT", s2)])

        def emit_S(h, qt, kt):
            s2 = (h * NQT + qt) % 2
            b0 = rot.next()
            off = max(0, kt - 4 * qt) * 128
            for m in range(2):
                P.op("pe", "matmul", out=ps[:, b0 + m, off:512],
                     lhsT=kT[m * 64:(m + 1) * 64, kt * 128:(kt + 1) * 128],
                     rhs=qTt[s2][m * 64:(m + 1) * 64, off:512], start=True, stop=True,
                     reads=[("B_kT", kt // KPC), ("B_qT", s2)], writes=[("ps", b0 + m)], sig=(m == 1))
            r = bst["pt"] % 3
            bst["pt"] += 1
            pslots.append(r)
            P.op("act", "activation", out=Pt[r][:, :, off:512], in_=ps[:, b0:b0 + 2, off:512], func=AF.Exp,
                 scale=0.125, reads=[("ps", b0), ("ps", b0 + 1)], writes=[("B_P", r)])
            if kt >= 4 * qt:
                P.op("pool", "tensor_tensor", out=Pt[r][:, :, off:off + 128], in0=Pt[r][:, :, off:off + 128],
                     in1=tri3, op=ALU.mult, reads=[("B_P", r), ("tri_b",)], writes=[("B_P", r)])

        def emit_PV(h, qt, kt, nk):
            r = pslots.popleft()
            off = max(0, kt - 4 * qt) * 128
            last = (kt == nk - 1)
            for m in range(2):
                P.op("pe", "matmul", out=ps[:, 4 + m, off:512], lhsT=V[:, kt, :], rhs=Pt[r][:, m, off:512],
                     start=(kt == 0), stop=last, reads=[("B_V", kt // KPC), ("B_P", r)], writes=[("ps", 4 + m)],
                     sig=False)
            for m in range(2):
                P.op("pe", "matmul", out=ps[:, 6 + m, off:512], lhsT=C.ones_b[:], rhs=Pt[r][:, m, off:512],
                     start=(kt == 0), stop=last, reads=[("ones_b",), ("B_P", r)], writes=[("ps", 6 + m)],
                     sig=(m == 1))

        def make_tail(e_, h, qt):
            def tail():
                bq = rot.next()
                P.op("pe", "matmul", out=ps[:, bq, :], lhsT=C.ones_b[:], rhs=sq[e_][:], start=True, stop=True,
                     reads=[("ones_b",), ("B_sq", e_)], writes=[("ps", bq)])
                P.op("act", "activation", out=sd[:], in_=ps[:, bq, :], func=AF.Ln, scale=1.0 / 128.0,
                     bias=C.eps_col[:, 0:1], reads=[("ps", bq), ("eps_col",)], writes=[("B_sd",)])
                P.op("act", "activation", out=rs[:], in_=sd[:], func=AF.Exp, scale=-0.5,
                     reads=[("B_sd",)], writes=[("B_rs",)])
                P.op("dve", "scalar_tensor_tensor", out=ot[e_][:], in0=oo[e_][:], scalar=gsc, in1=rs[:],
                     op0=ALU.mult, op1=ALU.mult, reads=[("B_oo", e_), ("B_rs",), ("B_gs",)], writes=[("B_ot", e_)])
                P.dma("sp", C.mixT[h * 128:(h + 1) * 128, qt * 512:(qt + 1) * 512], ot[e_][:],
                      reads=[("B_ot", e_)], writes=[("mixT", h, qt)])
            return tail

        def epilogue(h, qt):
            e_ = bst["ep"] % 2
            bst["ep"] += 1
            P.op("act", "activation", out=r1[:], in_=ps[:, 6, :], func=AF.Copy, reads=[("ps", 6)], writes=[("B_r1",)])
            P.op("act", "activation", out=r2[:], in_=ps[:, 7, :], func=AF.Copy, reads=[("ps", 7)], writes=[("B_r2",)])
            P.op("dve", "tensor_copy", out=o1[:], in_=ps[:, 4, :], reads=[("ps", 4)], writes=[("B_o1",)])
            P.op("dve", "tensor_copy", out=o2[:], in_=ps[:, 5, :], reads=[("ps", 5)], writes=[("B_o2",)])
            P.op("dve", "reciprocal", out=r1[:], in_=r1[:], reads=[("B_r1",)], writes=[("B_r1",)])
            P.op("dve", "tensor_tensor", out=o1[:], in0=o1[:], in1=r1[:], op=ALU.mult,
                 reads=[("B_o1",), ("B_r1",)], writes=[("B_o1",)])
            P.op("dve", "reciprocal", out=r2[:], in_=r2[:], reads=[("B_r2",)], writes=[("B_r2",)])
            P.op("dve", "tensor_tensor", out=o2[:], in0=o2[:], in1=r2[:], op=ALU.mult,
                 reads=[("B_o2",), ("B_r2",)], writes=[("B_o2",)])
            P.op("dve", "scalar_tensor_tensor", out=oo[e_][:], in0=o2[:], scalar=neglam, in1=o1[:],
                 op0=ALU.mult, op1=ALU.add, reads=[("B_o1",), ("B_o2",), ("B_neglam",)], writes=[("B_oo", e_)])
            P.op("dve", "tensor_tensor", out=sq[e_][:], in0=oo[e_][:], in1=oo[e_][:], op=ALU.mult,
                 reads=[("B_oo", e_)], writes=[("B_sq", e_)])
            pending.append(make_tail(e_, h, qt))

        def b_iter(h, qt, kt, nk):
            def f():
                if kt + 1 < nk:
                    emit_S(h, qt, kt + 1)
                emit_PV(h, qt, kt, nk)
                if kt == min(7, nk - 1):
                    while pending:
                        pending.pop(0)()
            return f

        def b_start(h, qt):
            def f():
                if qt == 0:
                    load_head(h)
                if h == 0 and qt == 0:
                    load_q(0, 0)
                nidx = h * NQT + qt + 1
                if nidx < N_HEADS * NQT:
                    load_q(nidx // NQT, nidx % NQT)
                emit_S(h, qt, 0)
            return f

        b_items = []
        for h in range(N_HEADS):
            for qt in range(NQT):
                nk = 4 * (qt + 1)
                b_items.append((0.3, b_start(h, qt)))
                for kt in range(nk):
                    b_items.append((1.0, b_iter(h, qt, kt, nk)))
                b_items.append((2.0, (lambda h=h, qt=qt: epilogue(h, qt))))

        units = [(m, hf, q4) for m in range(4) for hf in range(NH) for q4 in range(4)]
        NU = len(units)
        kx = lambda xs, c, r: ("C_Xc", xs, c, r)
        s1banks = {}
        s4banks = {}

        def load_u(g4):
            m, hf = g4 // NH, g4 % NH
            P.dma("sp", uTt[g4 % 2][:], C.uT[m * 128:(m + 1) * 128, hf * TS:(hf + 1) * TS],
                  writes=[("C_uTt", g4 % 2)])

        RP = max(1, min(RG, 512 // J1))

        def S1_pe(u, r):
            m, hf, q4 = units[u]
            k = m * 4 + q4
            us = (u // 4) % 2
            b0 = rot.next()
            s1banks[(u, r)] = b0
            uv = uTt[us][:].rearrange("p (j r) -> p r j", r=R1)
            for ri in range(2):
                P.op("pe", "matmul", out=ps[:, b0 + ri, 0:RP * J1].rearrange("p (r j) -> p r j", r=RP),
                     lhsT=BT[:, k * 2 + ri, :], rhs=uv[:, r:r + RP, :],
                     start=True, stop=True, reads=[("C_BT", (k * 2 + ri) // 4), ("C_uTt", us)],
                     writes=[("ps", b0 + ri)], sig=(ri == 1))

        def S1_act(u, r):
            xs = u % 2
            b0 = s1banks.pop((u, r))
            P.op("act", "activation", out=X[xs][:, :, r:r + RP, :],
                 in_=ps[:, b0:b0 + 2, 0:RP * J1].rearrange("p c (r j) -> p c r j", r=RP), func=AF.Copy,
                 reads=[("ps", b0), ("ps", b0 + 1)],
                 writes=[kx(xs, c, rr) for c in range(2) for rr in range(r, r + RP)])

        def S2_gen(u):
            m, hf, q4 = units[u]
            k = m * 4 + q4
            xs = u % 2
            Xs = X[xs]
            sc = lambda T_, i: T_[:, i, k:k + 1]
            ops = []

            def stt(out, in0, scalar, in1, rd, wr):
                ops.append(lambda: P.op("dve", "scalar_tensor_tensor", out=out, in0=in0, scalar=scalar, in1=in1,
                                        op0=ALU.mult, op1=ALU.add, reads=rd, writes=wr))

            def cacc(dre, dim, sre, sim, pi, kdr, kdi, ksr, ksi, extra=()):
                wr, wi, wni = sc(PWR, pi), sc(PWI, pi), sc(PWN, pi)
                ex = list(extra)
                stt(dre, sre, wr, dre, [ksr, kdr] + ex, [kdr])
                stt(dim, sim, wr, dim, [ksi, kdi] + ex, [kdi])
                stt(dre, sim, wni, dre, [ksi, kdr] + ex, [kdr])
                stt(dim, sre, wi, dim, [ksr, kdi] + ex, [kdi])

            kc = ("C_carry", k)
            if hf > 0:
                cacc(Xs[:, 0, 0, 0:1], Xs[:, 1, 0, 0:1], carry[:, k, 0:1], carry[:, k, 1:2], 1,
                     kx(xs, 0, 0), kx(xs, 1, 0), kc, kc)
            for r in range(1, R1):
                cacc(Xs[:, 0, r, :], Xs[:, 1, r, :], Xs[:, 0, r - 1, :], Xs[:, 1, r - 1, :], 1,
                     kx(xs, 0, r), kx(xs, 1, r), kx(xs, 0, r - 1), kx(xs, 1, r - 1))
            klast = (kx(xs, 0, R1 - 1), kx(xs, 1, R1 - 1))
            src = (Xs[:, 0, R1 - 1, :], Xs[:, 1, R1 - 1, :])
            srck = klast
            Xtop = Xs[:, :, R1 - 1, :]
            src2 = Xtop
            for s_ in range(NS):
                d = 1 << s_
                if s_ == NS - 1:
                    dst, dstk, dst2 = (Xs[:, 0, R1 - 1, :], Xs[:, 1, R1 - 1, :]), klast, Xtop
                else:
                    hb = HS[s_ % 2]
                    dst, dstk, dst2 = (hb[:, 0, :], hb[:, 1, :]), (("C_hs", s_ % 2, 0), ("C_hs", s_ % 2, 1)), hb[:]
                wr, wi, wni = sc(PWR, R1 + s_), sc(PWI, R1 + s_), sc(PWN, R1 + s_)
                ops.append(lambda dst2=dst2, src2=src2, d=d, srck=srck, dstk=dstk: P.op(
                    "dve", "tensor_copy", out=dst2[:, :, 0:d], in_=src2[:, :, 0:d], reads=list(srck), writes=list(dstk)))
                n = J1 - d
                stt(dst[0][:, d:J1], src[0][:, 0:n], wr, src[0][:, d:J1], list(srck), [dstk[0]])
                stt(dst[1][:, d:J1], src[1][:, 0:n], wr, src[1][:, d:J1], list(srck), [dstk[1]])
                stt(dst[0][:, d:J1], src[1][:, 0:n], wni, dst[0][:, d:J1], list(srck) + [dstk[0]], [dstk[0]])
                stt(dst[1][:, d:J1], src[0][:, 0:n], wi, dst[1][:, d:J1], list(srck) + [dstk[1]], [dstk[1]])
                src, srck, src2 = dst, dstk, dst2
            if hf + 1 < NH:
                ops.append(lambda: P.op("dve", "tensor_copy", out=carry[:, k, :], in_=Xs[:, :, R1 - 1, J1 - 1],
                                        reads=list(klast), writes=[kc]))
            for r in range(R1 - 1):
                cacc(Xs[:, 0, r, 1:J1], Xs[:, 1, r, 1:J1], Xs[:, 0, R1 - 1, 0:J1 - 1], Xs[:, 1, R1 - 1, 0:J1 - 1],
                     r + 1, kx(xs, 0, r), kx(xs, 1, r), klast[0], klast[1])
            return ops

        def S3(u, rg):
            xs = u % 2
            rs_ = slice(rg * RG, (rg + 1) * RG)
            P.op("act", "activation", out=Xb[rg % 2][:], in_=X[xs][:, :, rs_, :], func=AF.Copy,
                 reads=[kx(xs, c, r) for c in range(2) for r in range(rg * RG, (rg + 1) * RG)]
                 + [kx(xs, 0, R1 - 1), kx(xs, 1, R1 - 1)], writes=[("C_Xb", rg % 2)])

        def S4_pe(u, r):
            m, hf, q4 = units[u]
            k = m * 4 + q4
            b = rot.next()
            s4banks[(u, r)] = b
            rg = r // RG
            for ri in range(2):
                P.op("pe", "matmul", out=ps[:, b, 0:RP * J1].rearrange("p (r j) -> p r j", r=RP),
                     lhsT=CT[:, k * 2 + ri, :], rhs=Xb[rg % 2][:, ri, r % RG:r % RG + RP, :],
                     start=(ri == 0), stop=(ri == 1), reads=ctf + [("C_Xb", rg % 2)], writes=[("ps", b)],
                     sig=(ri == 1))

        def S4_act(u, r):
            m, hf, q4 = units[u]
            b = s4banks.pop((u, r))
            P.op("act", "activation", out=yraw[q4 * 32:(q4 + 1) * 32, r:r + RP, :],
                 in_=ps[q4 * 32:(q4 + 1) * 32, b, 0:RP * J1].rearrange("p (r j) -> p r j", r=RP),
                 func=AF.Copy, reads=[("ps", b)], writes=[("C_yraw", q4, rr) for rr in range(r, r + RP)])

        gst = dict(rr=0)
        gslot = {}
        NGS = 4
        ta_ = ta_ + [sb("C_ga%d" % i, [128, J1], F32) for i in range(2, NGS)]
        tb_2 = tb_2 + [sb("C_gb%d" % i, [128, J1], F32) for i in range(2, NGS)]
        go = go + [sb("C_go%d" % i, [128, J1], BF16) for i in range(2, NGS)]

        def G_a(g4, r):
            m, hf = g4 // NH, g4 % NH
            us = g4 % 2
            i = gst["rr"] % NGS
            gst["rr"] += 1
            gslot[(g4, r)] = i
            P.op("dve", "scalar_tensor_tensor", out=ta_[i][:], in0=uTt[us][:, r::R1], scalar=Dcol[:, m:m + 1],
                 in1=yraw[:, r, :], op0=ALU.mult, op1=ALU.add,
                 reads=[("C_uTt", us), ("C_D",)] + [("C_yraw", q, r) for q in range(4)], writes=[("C_ga", i)])
            P.op("pool", "tensor_tensor", out=tb_2[i][:], in0=ta_[i][:], in1=ta_[i][:], op=ALU.mult,
                 reads=[("C_ga", i)], writes=[("C_gb", i)])
            P.op("pool", "tensor_scalar", out=tb_2[i][:], in0=tb_2[i][:], scalar1=0.044715, scalar2=1.0,
                 op0=ALU.mult, op1=ALU.add, reads=[("C_gb", i)], writes=[("C_gb", i)])
            P.op("pool", "tensor_tensor", out=tb_2[i][:], in0=tb_2[i][:], in1=ta_[i][:], op=ALU.mult,
                 reads=[("C_gb", i), ("C_ga", i)], writes=[("C_gb", i)])

        def G_b(g4, r):
            m, hf = g4 // NH, g4 % NH
            i = gslot.pop((g4, r))
            P.op("act", "activation", out=tb_2[i][:], in_=tb_2[i][:], func=AF.Tanh, scale=math.sqrt(2.0 / math.pi),
                 reads=[("C_gb", i)], writes=[("C_gb", i)])
            P.op("pool", "tensor_scalar", out=tb_2[i][:], in0=tb_2[i][:], scalar1=1.0, scalar2=0.5, op0=ALU.add,
                 op1=ALU.mult, reads=[("C_gb", i)], writes=[("C_gb", i)])
            P.op("pool", "tensor_tensor", out=go[i][:], in0=ta_[i][:], in1=tb_2[i][:], op=ALU.mult,
                 reads=[("C_ga", i), ("C_gb", i)], writes=[("C_go", i)])
            col = hf * TS + r * J1
            P.dma("sp", C.gT[m * 128:(m + 1) * 128, col:col + J1], go[i][:], reads=[("C_go", i)],
                  writes=[("gT", m, hf, r)])

        def G(g4, r):
            G_a(g4, r)
            G_b(g4, r)

        NG = 16

        def c_group(step, g, ops, per):
            def f():
                u, prev, nxt = step, step - 1, step + 1
                if prev >= 0:
                    if g < R1 // RG:
                        S3(prev, g)
                    if 1 <= g <= 4:
                        for r in range(4 * (g - 1), 4 * g, RP):
                            S4_pe(prev, r)
                            S4_act(prev, r)
                    if prev % 4 == 3 and 8 <= g <= 15:
                        for r in range(2 * (g - 8), 2 * (g - 7)):
                            G_b(prev // 4, r)
                    if prev % 4 == 3 and 7 <= g <= 14:
                        for r in range(2 * (g - 7), 2 * (g - 6)):
                            G_a(prev // 4, r)
                if nxt < NU:
                    if g == 0 and nxt % 4 == 0:
                        load_u(nxt // 4)
                    if 6 <= g <= 13:
                        for r in range(2 * (g - 6), 2 * (g - 5)):
                            if r % RP == 0:
                                S1_pe(nxt, r)
                                S1_act(nxt, r)
                for o in ops[g * per:(g + 1) * per]:
                    o()
            return f

        load_u(0)
        for r in range(0, R1, RP):
            S1_pe(0, r)
            S1_act(0, r)
        c_items = []
        import os
        if os.environ.get("BC_NOSKEW"):
            def unit_all(u):
                def f():
                    if u > 0:
                        if u % 4 == 0:
                            load_u(u // 4)
                        for r in range(0, R1, RP):
                            S1_pe(u, r)
                            S1_act(u, r)
                    for o in S2_gen(u):
                        o()
                    for rg in range(R1 // RG):
                        S3(u, rg)
                        for r in range(rg * RG, (rg + 1) * RG, RP):
                            S4_pe(u, r)
                            S4_act(u, r)
                    if u % 4 == 3:
                        for r in range(R1):
                            G(u // 4, r)
                return f
            c_items = [(1.0, unit_all(u)) for u in range(NU)]
        for step in (range(NU + 1) if not os.environ.get("BC_NOSKEW") else []):
            ops = S2_gen(step) if step < NU else []
            per = -(-len(ops) // NG) if ops else 0
            for g in range(NG):
                c_items.append((1.0, c_group(step, g, ops, per)))

        import os
        if os.environ.get("BC_SKIP_B"):
            b_items = []
        nb, ncn = len(b_items), len(c_items)
        ib = ic = 0
        while ib < nb or ic < ncn:
            if ic < ncn and (ib >= nb or ic * nb <= ib * ncn * 1.06):
                c_items[ic][1]()
                ic += 1
            else:
                b_items[ib][1]()
                ib += 1
        while pending:
            pending.pop(0)()
        P.barrier()
        P.flush()
    with ExitStack() as st:
        sb = lambda n, s, d: st.enter_context(nc.sbuf_tensor("%s_l%d" % (n, l), s, d))
        GW = sb("C_GW", [128, 4, 512], BF16)
        gb = sb("C_gbias", [128, 4], F32)
        gt = [sb("C_gt%d" % i, [128, 4, J1], BF16) for i in range(2)]
        sg = [sb("C_sg%d" % i, [128, 512], F32) for i in range(2)]
        SO = sb("C_SO", [128, 4, L], BF16)
        P.dma("pool", GW[:], C.glu_w[l].rearrange("(kt p) n -> p kt n", p=128), writes=[("C_GW",)])
        P.dma("sp", gb[:], C.glu_b[l].rearrange("(m p) -> p m", p=128), writes=[("C_gbias",)],
              allow_slow_non_contiguous=True)
        rr = 0
        for blk in range(NH * R1):
            hf, r = blk // R1, blk % R1
            i = blk % 2
            P.dma("sp", gt[i][:], C.gT[:, blk * J1:(blk + 1) * J1].rearrange("(m p) t -> p m t", p=128),
                  writes=[("C_gt", i)])
            for mo in range(4):
                b = rr % 4
                rr += 1
                for kt in range(4):
                    P.op("pe", "matmul", out=ps[:, b, 0:J1], lhsT=GW[:, kt, mo * 128:(mo + 1) * 128],
                         rhs=gt[i][:, kt, :], start=(kt == 0), stop=(kt == 3),
                         reads=[("C_GW",), ("C_gt", i)], writes=[("ps", b)], sig=(kt == 3))
                j = rr % 2
                P.op("act", "activation", out=sg[j][:, 0:J1], in_=ps[:, b, 0:J1], func=AF.Sigmoid,
                     bias=gb[:, mo:mo + 1], reads=[("ps", b), ("C_gbias",)], writes=[("C_sg", j)])
                P.op("dve", "tensor_tensor", out=SO[:, mo, hf * TS + r:(hf + 1) * TS:R1], in0=gt[i][:, mo, :],
                     in1=sg[j][:, 0:J1], op=ALU.mult, reads=[("C_gt", i), ("C_sg", j)], writes=[("C_SO", mo, blk)])
        for mo in range(4):
            for hf in range(2):
                sl = slice(hf * (L // 2), (hf + 1) * (L // 2))
                P.dma("sp", C.mixT[512 + mo * 128:512 + (mo + 1) * 128, sl], SO[:, mo, sl],
                      reads=[("C_SO", mo, blk) for blk in range(NH * R1)], writes=[("mixT", 4, mo, hf)])
        P.barrier()
        P.flush()


def resid_ln(P, C, tag, i, banks, xres, xres_keys, gt, bt, r, st6, mv, sd, out_ap, out_keys):
    ps = C.ps
    for nb in range(2):
        P.op("dve", "scalar_tensor_tensor", out=r[:, nb * 512:(nb + 1) * 512], in0=xres[:, nb * 512:(nb + 1) * 512],
             scalar=ALPHA, in1=ps[:, banks[nb], :], op0=ALU.mult, op1=ALU.add,
             reads=list(xres_keys) + [("ps", banks[nb])], writes=[(tag + "_r", i, nb)])
        P.op("dve", "bn_stats", out=st6[:, nb, :], in_=r[:, nb * 512:(nb + 1) * 512],
             reads=[(tag + "_r", i, nb)], writes=[(tag + "_st", i, nb)])
    P.op("dve", "bn_aggr", out=mv[:, 0:2], in_=st6[:].rearrange("p a b -> p (a b)"),
         reads=[(tag + "_st", i, 0), (tag + "_st", i, 1)], writes=[(tag + "_mv", i)])
    P.op("act", "activation", out=sd[:, 0:1], in_=mv[:, 1:2], func=AF.Sqrt, bias=C.eps_col[:, 1:2],
         reads=[(tag + "_mv", i), ("eps_col",)], writes=[(tag + "_sd", i)])
    P.op("dve", "reciprocal", out=sd[:, 1:2], in_=sd[:, 0:1], reads=[(tag + "_sd", i)], writes=[(tag + "_rstd", i)])
    rk = [(tag + "_r", i, 0), (tag + "_r", i, 1)]
    P.op("dve", "scalar_tensor_tensor", out=r[:], in0=r[:], scalar=mv[:, 0:1], in1=gt[:], op0=ALU.subtract,
         op1=ALU.mult, reads=rk + [(tag + "_mv", i), (tag + "_g",)], writes=rk)
    P.op("dve", "scalar_tensor_tensor", out=out_ap, in0=r[:], scalar=sd[:, 1:2], in1=bt[:], op0=ALU.mult,
         op1=ALU.add, reads=rk + [(tag + "_rstd", i), (tag + "_b",)], writes=list(out_keys))


def phase_D1(P, nc, C, l, x_src):
    L = C.L
    NMT = L // 512
    ps, psb = C.ps, C.psb
    with ExitStack() as st:
        sb = lambda n, s, d: st.enter_context(nc.sbuf_tensor("%s_l%d" % (n, l), s, d))
        Wo = sb("D1_w", [128, 8, 1024], BF16)
        gt = sb("D1_g", [128, 1024], F32)
        bt = sb("D1_b", [128, 1024], F32)
        mixt = [sb("D1_mix%d" % i, [128, 8, 512], BF16) for i in range(2)]
        xr = [sb("D1_xr%d" % i, [128, 4, 1024], F32) for i in range(2)]
        x1o = [sb("D1_x1o%d" % i, [128, 4, 1024], F32) for i in range(2)]
        r = [sb("D1_r%d" % i, [128, 1024], F32) for i in range(4)]
        x1b = [sb("D1_x1b%d" % i, [128, 1024], BF16) for i in range(4)]
        x1T = [sb("D1_x1T%d" % i, [128, 8, 512], BF16) for i in range(2)]
        st6 = [sb("D1_st%d" % i, [128, 2, 6], F32) for i in range(4)]
        mv = [sb("D1_mv%d" % i, [128, 2], F32) for i in range(4)]
        sd = [sb("D1_sd%d" % i, [128, 2], F32) for i in range(4)]
        wosrc = C.w_out[l].rearrange("(kt p) n -> p kt n", p=128)
        for cb in range(2):
            P.dma("pool", Wo[:, :, cb * 512:(cb + 1) * 512], wosrc[:, :, cb * 512:(cb + 1) * 512],
                  writes=[("D1_w", cb)])
        P.dma("sp", gt[:].unsqueeze(1), C.ln1_g[l:l + 1, :].partition_broadcast(128), writes=[("D1_g",)])
        P.dma("sp", bt[:].unsqueeze(1), C.ln1_b[l:l + 1, :].partition_broadcast(128), writes=[("D1_b",)])

        def load(mt):
            s = mt % 2
            sl = slice(mt * 512, (mt + 1) * 512)
            P.dma("sp", mixt[s][:], C.mixT[:, sl].rearrange("(kt p) t -> p kt t", p=128),
                  writes=[("D1_mix", s)])
            P.dma("sp", xr[s][:], x_src[sl, :].rearrange("(s p) d -> p s d", p=128),
                  reads=[("xs", 2 * mt), ("xs", 2 * mt + 1)], writes=[("D1_xr", s)])

        load(0)
        cnt = dict(mb=0, tbr=0)
        tiles = [(mt, sub) for mt in range(NMT) for sub in range(4)]
        NT_ = len(tiles)

        def stage1(t):
            mt, sub = tiles[t]
            s = mt % 2
            i = t % 4
            if sub == 0 and mt + 1 < NMT:
                load(mt + 1)
            banks = []
            for nb in range(2):
                b = 2 + cnt["mb"] % 6
                cnt["mb"] += 1
                banks.append(b)
                for kt in range(8):
                    P.op("pe", "matmul", out=ps[:, b, :], lhsT=mixt[s][:, kt, sub * 128:(sub + 1) * 128],
                         rhs=Wo[:, kt, nb * 512:(nb + 1) * 512], start=(kt == 0), stop=(kt == 7),
                         reads=[("D1_mix", s), ("D1_w", nb)], writes=[("ps", b)], sig=(kt == 7))
            resid_ln(P, C, "D1", i, banks, xr[s][:, sub, :], [("D1_xr", s)], gt, bt, r[i], st6[i], mv[i], sd[i],
                     x1o[s][:, sub, :], [("D1_x1o", s, sub)])
            P.op("act", "activation", out=x1b[i][:], in_=x1o[s][:, sub, :], func=AF.Copy,
                 reads=[("D1_x1o", s, sub)], writes=[("D1_x1b", i)])

        def stage2(t):
            mt, sub = tiles[t]
            s = mt % 2
            i = t % 4
            tb = cnt["tbr"] % 2
            cnt["tbr"] += 1
            for kt in range(8):
                P.op("pe", "transpose", out=psb[:, tb, kt * 128:(kt + 1) * 128],
                     in_=x1b[i][:, kt * 128:(kt + 1) * 128], identity=C.ident_b[:],
                     reads=[("D1_x1b", i), ("ident_b",)], writes=[("psT", tb)], sig=(kt == 7))
            P.op("act", "activation", out=x1T[s][:, :, sub * 128:(sub + 1) * 128],
                 in_=psb[:, tb, :].rearrange("p (k t) -> p k t", k=8), func=AF.Copy,
                 reads=[("psT", tb)], writes=[("D1_x1T", s, sub)])
            if sub == 3:
                sl = slice(mt * 512, (mt + 1) * 512)
                P.dma("sp", C.x1[sl, :].rearrange("(s p) d -> p s d", p=128), x1o[s][:],
                      reads=[("D1_x1o", s, q) for q in range(4)], writes=[("x1", 2 * mt), ("x1", 2 * mt + 1)])
                P.dma("sp", C.x1T[:, sl].rearrange("(kt p) t -> p kt t", p=128), x1T[s][:],
                      reads=[("D1_x1T", s, q) for q in range(4)], writes=[("x1T", 2 * mt), ("x1T", 2 * mt + 1)])

        LOOK = 2
        for t in range(min(LOOK, NT_)):
            stage1(t)
        for t in range(NT_):
            if t + LOOK < NT_:
                stage1(t + LOOK)
            stage2(t)
        P.barrier()
        P.flush()


def phase_D2(P, nc, C, l, out_dst, out_tag):
    L = C.L
    NT2 = L // 256
    ps = C.ps
    with ExitStack() as st:
        sb = lambda n, s, d: st.enter_context(nc.sbuf_tensor("%s_l%d" % (n, l), s, d))
        W1 = sb("D2_w1", [128, 8, 4096], BF16)
        W2 = sb("D2_w2", [128, 32, 1024], BF16)
        gt = sb("D2_g", [128, 1024], F32)
        bt = sb("D2_b", [128, 1024], F32)
        xT = [sb("D2_xT%d" % i, [128, 8, 256], BF16) for i in range(2)]
        xr = [sb("D2_xr%d" % i, [128, 2, 1024], F32) for i in range(2)]
        hT = sb("D2_hT", [128, 32, 256], BF16)
        rl = [sb("D2_rl%d" % i, [128, 512], BF16) for i in range(2)]
        r = [sb("D2_r%d" % i, [128, 1024], F32) for i in range(2)]
        yo = [sb("D2_yo%d" % i, [128, 1024], F32) for i in range(2)]
        st6 = [sb("D2_st%d" % i, [128, 2, 6], F32) for i in range(2)]
        mv = [sb("D2_mv%d" % i, [128, 2], F32) for i in range(2)]
        sd = [sb("D2_sd%d" % i, [128, 2], F32) for i in range(2)]
        w1src = C.w_ff1[l].rearrange("(kt p) n -> p kt n", p=128)
        for cb in range(8):
            P.dma("pool", W1[:, :, cb * 512:(cb + 1) * 512], w1src[:, :, cb * 512:(cb + 1) * 512],
                  writes=[("D2_w1", cb)])
        for kt in range(32):
            P.dma("pool", W2[:, kt, :], C.w_ff2[l, kt * 128:(kt + 1) * 128, :], writes=[("D2_w2", kt)])
        P.dma("sp", gt[:].unsqueeze(1), C.ln2_g[l:l + 1, :].partition_broadcast(128), writes=[("D2_g",)])
        P.dma("sp", bt[:].unsqueeze(1), C.ln2_b[l:l + 1, :].partition_broadcast(128), writes=[("D2_b",)])
        w1k = [("D2_w1", kt) for kt in range(8)]

        def load(t2):
            s = t2 % 2
            sl = slice(t2 * 256, (t2 + 1) * 256)
            P.dma("sp", xT[s][:], C.x1T[:, sl].rearrange("(kt p) t -> p kt t", p=128),
                  reads=[("x1T", t2)], writes=[("D2_xT", s)])
            P.dma("sp", xr[s][:], C.x1[sl, :].rearrange("(s p) d -> p s d", p=128),
                  reads=[("x1", t2)], writes=[("D2_xr", s)])

        load(0)
        ub = 0
        it = 0
        for t2 in range(NT2):
            s = t2 % 2
            if t2 + 1 < NT2:
                load(t2 + 1)
            for fp in range(16):
                b = ub % 4
                j = ub % 2
                ub += 1
                for half in range(2):
                    ft = fp * 2 + half
                    for kt in range(8):
                        P.op("pe", "matmul", out=ps[:, b, half * 256:(half + 1) * 256],
                             lhsT=W1[:, kt, ft * 128:(ft + 1) * 128], rhs=xT[s][:, kt, :],
                             start=(kt == 0), stop=(kt == 7), reads=[("D2_xT", s), ("D2_w1", ft // 4)],
                             writes=[("ps", b)], sig=(kt == 7 and half == 1))
                P.op("act", "activation", out=rl[j][:], in_=ps[:, b, :], func=AF.Relu,
                     reads=[("ps", b)], writes=[("D2_rl", j)])
                P.op("pool", "tensor_tensor", out=hT[:, fp * 2:fp * 2 + 2, :],
                     in0=rl[j][:].rearrange("p (a t) -> p a t", a=2), in1=rl[j][:].rearrange("p (a t) -> p a t", a=2),
                     op=ALU.mult, reads=[("D2_rl", j)], writes=[("D2_hT", fp)])
            hk = [("D2_hT", fp) for fp in range(16)]
            for sub in range(2):
                i = it % 2
                it += 1
                banks = [4 + sub * 2, 5 + sub * 2]
                for nb in range(2):
                    b = banks[nb]
                    for kt in range(32):
                        P.op("pe", "matmul", out=ps[:, b, :], lhsT=hT[:, kt, sub * 128:(sub + 1) * 128],
                             rhs=W2[:, kt, nb * 512:(nb + 1) * 512], start=(kt == 0), stop=(kt == 31),
                             reads=hk + [("D2_w2", kt)], writes=[("ps", b)], sig=(kt == 31))
                resid_ln(P, C, "D2", i, banks, xr[s][:, sub, :], [("D2_xr", s)], gt, bt, r[i], st6[i], mv[i], sd[i],
                         yo[i][:], [("D2_yo", i)])
                row = t2 * 256 + sub * 128
                P.dma("sp", out_dst[row:row + 128, :], yo[i][:], reads=[("D2_yo", i)], writes=[(out_tag, t2)])
        P.barrier()
        P.flush()


def build(L=8192, phases=None, dbg=()):
    nc = bass.Bass("TRN2", target_bir_lowering=False)
    C = declare_dram(nc, L, dbg)
    with ExitStack() as st:
        P = Prog(nc, st)
        C.ps_t = st.enter_context(nc.psum_tensor("ps", [128, 8, 512], F32))
        C.ps = C.ps_t
        C.psb = C.ps_t[:].bitcast(BF16)
        C.ident_b = st.enter_context(nc.sbuf_tensor("ident_b", [128, 128], BF16))
        C.tri_b = st.enter_context(nc.sbuf_tensor("tri_b", [128, 128], BF16))
        C.ones_b = st.enter_context(nc.sbuf_tensor("ones_b", [128, 128], BF16))
        C.ident_f = st.enter_context(nc.sbuf_tensor("ident_f", [128, 128], F32))
        P.dma("pool", C.ident_b[:], C.c_misc[0], writes=[("ident_b",)])
        P.dma("pool", C.tri_b[:], C.c_misc[1], writes=[("tri_b",)])
        P.dma("pool", C.ones_b[:], C.c_misc[2], writes=[("ones_b",)])
        P.dma("sp", C.ident_f[:], C.c_misc[0], writes=[("ident_f",)])
        C.eps_col = st.enter_context(nc.sbuf_tensor("eps_col", [128, 4], F32))
        P.op("pool", lambda e: e.memset(C.eps_col[:, 0:1], RMS_EPS), writes=[("eps_col",)])
        P.op("pool", lambda e: e.memset(C.eps_col[:, 1:2], LN_EPS), writes=[("eps_col",)])
        P.barrier()
        P.flush()
        for l in range(DEPTH):
            x_src = C.x if l == 0 else C.xs
            if phases is None or ("A", l) in phases:
                phase_A(P, nc, C, l, x_src)
            if phases is None or ("BC", l) in phases:
                phase_BC(P, nc, C, l)
            if phases is not None and ("B", l) in phases:
                phase_B(P, nc, C, l)
            if phases is not None and ("C", l) in phases:
                phase_C(P, nc, C, l)
            if phases is None or ("D1", l) in phases:
                phase_D1(P, nc, C, l, x_src)
            if phases is None or ("D2", l) in phases:
                phase_D2(P, nc, C, l, C.xs if l == 0 else C.y, "xs" if l == 0 else "y")
    return nc


_NC_CACHE = {}
_IN_NAMES = ["w_in", "w_out", "lam_qk", "subln_g", "ssm_lam_re", "ssm_lam_im", "ssm_log_dt", "ssm_b_re",
             "ssm_b_im", "ssm_c_re", "ssm_c_im", "ssm_d", "glu_w", "glu_b", "ln1_g", "ln1_b", "w_ff1", "w_ff2",
             "ln2_g", "ln2_b"]


def kernel(**inputs):
    x = np.ascontiguousarray(np.asarray(inputs["x"], dtype=np.float32))
    B, L, D = x.shape
    if L not in _NC_CACHE:
        _NC_CACHE[L] = build(L)
    nc = _NC_CACHE[L]
    consts = make_consts(L)
    shared = {k: np.ascontiguousarray(np.asarray(inputs[k], dtype=np.float32)) for k in _IN_NAMES}
    shared.update(consts)
    in_maps = []
    for b in range(B):
        m = dict(shared)
        m["x"] = x[b]
        in_maps.append(m)
    res = run_bass_kernel_spmd(nc, in_maps, core_ids=list(range(B)))
    out = np.stack([np.asarray(r["y"], dtype=np.float32).reshape(L, D) for r in res.results], axis=0)
    return out
```

```python
import math
from contextlib import ExitStack

import numpy as np
import ml_dtypes

import concourse.bass as bass
import concourse.mybir as mybir
from concourse.bass_utils import run_bass_kernel_spmd

F32 = mybir.dt.float32
BF16 = mybir.dt.bfloat16
AF = mybir.ActivationFunctionType
ALU = mybir.AluOpType
AX = mybir.AxisListType

D_MODEL = 1024
DEPTH = 2
N_HEADS = 4
D_FF = 4096
ALPHA = (2.0 * DEPTH) ** 0.25
LN_EPS = 1e-5
RMS_EPS = 1e-5
ROPE_THETA = 10000.0


class Ev:
    __slots__ = ("sem", "val")

    def __init__(self, sem, val):
        self.sem = sem
        self.val = val


class Prog:
    ENGS = ("pe", "act", "dve", "pool", "sp")

    def __init__(self, nc, stack, n_dma_sems=24):
        self.nc = nc
        self.q = {e: [] for e in self.ENGS}
        self.esem = {}
        self.tick = {}
        for e in ("pe", "act", "dve", "pool"):
            self.esem[e] = stack.enter_context(nc.semaphore("sem_" + e))
            self.tick[e] = 0
        self.dsem = {}
        self.dcnt = {}
        self.drr = {}
        for qn in ("sp", "pool", "act"):
            n = n_dma_sems if qn == "sp" else 8
            self.dsem[qn] = [stack.enter_context(nc.semaphore("dq_%s_%d" % (qn, i))) for i in range(n)]
            self.dcnt[qn] = [0] * n
            self.drr[qn] = 0
        self.seen = {e: {} for e in self.ENGS}
        self.last_w = {}
        self.readers = {}
        self.all_sems = {}

    def _deps(self, reads, writes, deps):
        out = list(deps)
        for k in reads:
            ev = self.last_w.get(k)
            if ev is not None:
                out.append(ev)
        for k in writes:
            ev = self.last_w.get(k)
            if ev is not None:
                out.append(ev)
            out.extend(self.readers.get(k, ()))
        return out

    def _update(self, reads, writes, ev):
        for k in reads:
            self.readers.setdefault(k, []).append(ev)
        for k in writes:
            self.last_w[k] = ev
            self.readers[k] = []

    def _emit_waits(self, eng, deps, skip_sem=None):
        seen = self.seen[eng]
        need = {}
        for ev in deps:
            if ev is None:
                continue
            if skip_sem is not None and ev.sem is skip_sem:
                continue
            sid = id(ev.sem)
            if seen.get(sid, 0) >= ev.val:
                continue
            if sid not in need or need[sid].val < ev.val:
                need[sid] = ev
        for sid, ev in need.items():
            seen[sid] = ev.val
            self.q[eng].append(("wait", ev.sem, ev.val))

    def op(self, eng, fn, reads=(), writes=(), deps=(), sig=True, **kw):
        if isinstance(fn, str):
            name = fn
            fn = lambda e, name=name, kw=kw: getattr(e, name)(**kw)
        d = self._deps(reads, writes, deps)
        self._emit_waits(eng, d, skip_sem=self.esem[eng] if eng == "pe" else None)
        if sig:
            self.tick[eng] += 1
            ev = Ev(self.esem[eng], self.tick[eng])
            self.q[eng].append(("op", fn, self.esem[eng], 1))
        else:
            ev = Ev(self.esem[eng], self.tick[eng] + 1)
            self.q[eng].append(("op", fn, None, 0))
        self._update(reads, writes, ev)
        return ev

    def dma(self, qn, out, in_, reads=(), writes=(), deps=(), **kw):
        d = self._deps(reads, writes, deps)
        i = self.drr[qn]
        self.drr[qn] = (i + 1) % len(self.dsem[qn])
        sem = self.dsem[qn][i]
        prev = self.dcnt[qn][i]
        if prev:
            d.append(Ev(sem, prev))
        self._emit_waits(qn, d)
        self.dcnt[qn][i] = prev + 16
        ev = Ev(sem, prev + 16)
        self.q[qn].append(("dma", out, in_, sem, kw))
        self._update(reads, writes, ev)
        return ev

    def barrier(self):
        evs = []
        for e in ("pe", "act", "dve", "pool"):
            if self.tick[e]:
                evs.append(Ev(self.esem[e], self.tick[e]))
        for qn in ("sp", "pool", "act"):
            for s, c in zip(self.dsem[qn], self.dcnt[qn]):
                if c:
                    evs.append(Ev(s, c))
        for e in self.ENGS:
            self._emit_waits(e, evs)

    def flush(self):
        nc = self.nc
        q = self.q

        def replay(eng, items):
            for it in items:
                if it[0] == "wait":
                    eng.wait_ge(it[1], it[2])
                elif it[0] == "op":
                    ins = it[1](eng)
                    if it[2] is not None:
                        ins.then_inc(it[2], it[3])
                else:
                    _, out, in_, sem, kw = it
                    eng.dma_start(out=out, in_=in_, **kw).then_inc(sem, 16)

        with nc.Block() as blk:
            @blk.tensor
            def _(e):
                replay(e, q["pe"])

            @blk.scalar
            def _(e):
                replay(e, q["act"])

            @blk.vector
            def _(e):
                replay(e, q["dve"])

            @blk.gpsimd
            def _(e):
                replay(e, q["pool"])

            @blk.sync
            def _(e):
                replay(e, q["sp"])
        self.q = {e: [] for e in self.ENGS}


def make_consts(L):
    pos = np.arange(L, dtype=np.float32)
    inv_freq = (ROPE_THETA ** (-np.arange(0, 64, 2, dtype=np.float32) / 64)).astype(np.float32)
    ang = (pos[:, None] * inv_freq[None, :]).astype(np.float32)
    cos = np.cos(ang.astype(np.float64)).astype(np.float32)
    sin = np.sin(ang.astype(np.float64)).astype(np.float32)
    cc = np.concatenate([cos, cos], axis=1)
    ss = np.concatenate([-sin, sin], axis=1)
    rope = np.stack([cc, ss], axis=1).astype(np.float32)
    ident = np.eye(128, dtype=np.float32)
    tri = (np.arange(128)[:, None] <= np.arange(128)[None, :]).astype(np.float32)
    ones = np.ones((128, 128), dtype=np.float32)
    misc = np.stack([ident, tri, ones], axis=0)
    return {"c_rope": rope, "c_misc": misc}


class Ctx:
    pass


def declare_dram(nc, L, dbg=()):
    C = Ctx()
    C.L = L

    def inp(name, shape, dt=F32):
        return nc.dram_tensor(name, list(shape), dt, kind="ExternalInput").ap()

    def scr(name, shape, dt):
        kind = "ExternalOutput" if name in dbg else "Internal"
        return nc.dram_tensor(name, list(shape), dt, kind=kind).ap()

    C.x = inp("x", [L, D_MODEL])
    C.w_in = inp("w_in", [DEPTH, 1024, 2048])
    C.w_out = inp("w_out", [DEPTH, 1024, 1024])
    C.lam_qk = inp("lam_qk", [DEPTH, 4, 64])
    C.subln_g = inp("subln_g", [DEPTH, 128])
    C.ssm_lam_re = inp("ssm_lam_re", [DEPTH, 32, 64])
    C.ssm_lam_im = inp("ssm_lam_im", [DEPTH, 32, 64])
    C.ssm_log_dt = inp("ssm_log_dt", [DEPTH, 32])
    C.ssm_b_re = inp("ssm_b_re", [DEPTH, 32, 64, 16])
    C.ssm_b_im = inp("ssm_b_im", [DEPTH, 32, 64, 16])
    C.ssm_c_re = inp("ssm_c_re", [DEPTH, 32, 16, 64])
    C.ssm_c_im = inp("ssm_c_im", [DEPTH, 32, 16, 64])
    C.ssm_d = inp("ssm_d", [DEPTH, 32, 16])
    C.glu_w = inp("glu_w", [DEPTH, 512, 512])
    C.glu_b = inp("glu_b", [DEPTH, 512])
    C.ln1_g = inp("ln1_g", [DEPTH, 1024])
    C.ln1_b = inp("ln1_b", [DEPTH, 1024])
    C.w_ff1 = inp("w_ff1", [DEPTH, 1024, 4096])
    C.w_ff2 = inp("w_ff2", [DEPTH, 4096, 1024])
    C.ln2_g = inp("ln2_g", [DEPTH, 1024])
    C.ln2_b = inp("ln2_b", [DEPTH, 1024])
    C.c_rope = inp("c_rope", [L, 2, 64])
    C.c_misc = inp("c_misc", [3, 128, 128])
    C.y = nc.dram_tensor("y", [L, D_MODEL], F32, kind="ExternalOutput").ap()
    C.qT = scr("s_qT", [4, 128, L], BF16)
    C.kT = scr("s_kT", [4, 128, L], BF16)
    C.v = scr("s_v", [L, 512], BF16)
    C.uT = scr("s_uT", [512, L], BF16)
    C.mixT = scr("s_mixT", [1024, L], BF16)
    C.x1 = scr("s_x1", [L, 1024], F32)
    C.x1T = scr("s_x1T", [1024, L], BF16)
    C.xs = scr("s_xs", [L, 1024], F32)
    C.gT = scr("s_gT", [512, L], BF16)
    return C


def phase_A(P, nc, C, l, x_src):
    L = C.L
    NMT = L // 512
    NT = L // 128
    ps = C.ps
    psb = C.psb
    with ExitStack() as st:
        sb = lambda n, s, d: st.enter_context(nc.sbuf_tensor("%s_l%d" % (n, l), s, d))
        W = sb("A_w", [128, 8, 2048], BF16)
        rope = sb("A_rope", [128, NT, 2, 64], F32)
        xb = [sb("A_xb%d" % i, [128, 4, 1024], BF16) for i in range(2)]
        xT = [sb("A_xT%d" % i, [128, 8, 512], BF16) for i in range(2)]
        uo = [sb("A_uo%d" % i, [128, 4, 512], BF16) for i in range(2)]
        vo = [sb("A_vo%d" % i, [128, 4, 512], BF16) for i in range(2)]
        qr = [sb("A_qr%d" % i, [128, 2, 512], BF16) for i in range(2)]
        t1 = [sb("A_t1%d" % i, [128, 512], F32) for i in range(2)]
        t2 = [sb("A_t2%d" % i, [128, 512], F32) for i in range(2)]
        qTo = [sb("A_qTo%d" % i, [128, 2, 4, 512], BF16) for i in range(2)]
        ident = C.ident_b

        wisrc = C.w_in[l].rearrange("(kt p) n -> p kt n", p=128)
        for cb in (3, 0, 1, 2):
            P.dma("pool", W[:, :, cb * 512:(cb + 1) * 512], wisrc[:, :, cb * 512:(cb + 1) * 512],
                  writes=[("A_w", cb)])
        rsrc = C.c_rope.rearrange("(t p) a d -> p t a d", p=128)
        for i in range(0, NT, 16):
            j = min(NT, i + 16)
            P.dma("sp", rope[:, i:j], rsrc[:, i:j], writes=[("A_rope",)])

        def load_x(mt):
            s = mt % 2
            P.dma("pool", xb[s][:], x_src[mt * 512:(mt + 1) * 512, :].rearrange("(s p) d -> p s d", p=128),
                  writes=[("A_xb", s)])

        load_x(0)
        cnt = dict(tb=0, ub=0, qb=0, qr=0)
        info = {}

        def stage1(g):
            mt, sub = g // 4, g % 4
            s = mt % 2
            if sub == 0:
                if mt + 1 < NMT:
                    load_x(mt + 1)
                for sb_ in range(4):
                    b = cnt["tb"] % 2
                    cnt["tb"] += 1
                    for kt in range(8):
                        P.op("pe", "transpose", out=psb[:, b, kt * 128:(kt + 1) * 128],
                             in_=xb[s][:, sb_, kt * 128:(kt + 1) * 128], identity=ident[:],
                             reads=[("A_xb", s)], writes=[("psT", b)], sig=(kt == 7))
                    P.op("act", "activation", out=xT[s][:, :, sb_ * 128:(sb_ + 1) * 128],
                         in_=psb[:, b, :].rearrange("p (k t) -> p k t", k=8), func=AF.Copy,
                         reads=[("psT", b)], writes=[("A_xT", s, sb_)])
                for m in range(4):
                    b = 2 + cnt["ub"] % 2
                    cnt["ub"] += 1
                    for kt in range(8):
                        P.op("pe", "matmul", out=ps[:, b, :], lhsT=W[:, kt, 1536 + m * 128:1536 + (m + 1) * 128],
                             rhs=xT[s][:, kt, :], start=(kt == 0), stop=(kt == 7),
                             reads=[("A_w", 3)] + [("A_xT", s, i) for i in range(4)], writes=[("ps", b)], sig=(kt == 7))
                    P.op("act", "activation", out=uo[s][:, m, :], in_=ps[:, b, :], func=AF.Copy,
                         reads=[("ps", b)], writes=[("A_uo", s)])
                P.dma("sp", C.uT[:, mt * 512:(mt + 1) * 512].rearrange("(m p) t -> p m t", p=128), uo[s][:],
                      reads=[("A_uo", s)], writes=[("uT", mt)])
            tile = mt * 4 + sub
            banks = []
            for blk in range(3):
                b = 4 + cnt["qb"] % 4
                cnt["qb"] += 1
                banks.append(b)
                for kt in range(8):
                    P.op("pe", "matmul", out=ps[:, b, :], lhsT=xT[s][:, kt, sub * 128:(sub + 1) * 128],
                         rhs=W[:, kt, blk * 512:(blk + 1) * 512], start=(kt == 0), stop=(kt == 7),
                         reads=[("A_w", blk), ("A_xT", s, sub)], writes=[("ps", b)], sig=(kt == 7))
            P.op("act", "activation", out=vo[s][:, sub, :], in_=ps[:, banks[2], :], func=AF.Copy,
                 reads=[("ps", banks[2])], writes=[("A_vo", s)])
            r = cnt["qr"] % 2
            cnt["qr"] += 1
            info[g] = r
            for qk in range(2):
                b = banks[qk]
                src3 = ps[:, b, :].rearrange("p (h d) -> p h d", h=8)
                cc = rope[:, tile, 0:1, :].broadcast_to([128, 8, 64])
                ss = rope[:, tile, 1:2, :].broadcast_to([128, 8, 64])
                t1v = t1[qk][:].rearrange("p (h d) -> p h d", h=8)
                t2v = t2[qk][:].rearrange("p (h d) -> p h d", h=8)
                P.op("dve", "tensor_tensor", out=t1v, in0=src3, in1=cc, op=ALU.mult,
                     reads=[("ps", b), ("A_rope",)], writes=[("A_t1", qk)])
                P.op("dve", "tensor_tensor", out=t2v[:, :, 0:32], in0=src3[:, :, 32:64], in1=ss[:, :, 0:32],
                     op=ALU.mult, reads=[("ps", b), ("A_rope",)], writes=[("A_t2a", qk)])
                P.op("dve", "tensor_tensor", out=t2v[:, :, 32:64], in0=src3[:, :, 0:32], in1=ss[:, :, 32:64],
                     op=ALU.mult, reads=[("ps", b), ("A_rope",)], writes=[("A_t2b", qk)])
                P.op("dve", "tensor_tensor", out=qr[r][:, qk, :], in0=t1[qk][:], in1=t2[qk][:], op=ALU.add,
                     reads=[("A_t1", qk), ("A_t2a", qk), ("A_t2b", qk)], writes=[("A_qr", r, qk)])

        def stage2(g):
            mt, sub = g // 4, g % 4
            s = mt % 2
            r = info.pop(g)
            tb = cnt["tb"] % 2
            cnt["tb"] += 1
            for qk in range(2):
                for h in range(4):
                    P.op("pe", "transpose", out=psb[:, tb, (qk * 4 + h) * 128:(qk * 4 + h + 1) * 128],
                         in_=qr[r][:, qk, h * 128:(h + 1) * 128], identity=ident[:],
                         reads=[("A_qr", r, qk)], writes=[("psT", tb)], sig=(qk == 1 and h == 3))
            P.op("act", "activation", out=qTo[s][:, :, :, sub * 128:(sub + 1) * 128],
                 in_=psb[:, tb, :].rearrange("p (a h t) -> p a h t", a=2, h=4), func=AF.Copy,
                 reads=[("psT", tb)], writes=[("A_qTo", s)])
            if sub == 3:
                P.dma("sp", C.v[mt * 512:(mt + 1) * 512, :].rearrange("(s p) c -> p s c", p=128), vo[s][:],
                      reads=[("A_vo", s)], writes=[("v", mt)])
                P.dma("sp", C.qT[:, :, mt * 512:(mt + 1) * 512].rearrange("h p t -> p h t"), qTo[s][:, 0, :, :],
                      reads=[("A_qTo", s)], writes=[("qT", mt)])
                P.dma("sp", C.kT[:, :, mt * 512:(mt + 1) * 512].rearrange("h p t -> p h t"), qTo[s][:, 1, :, :],
                      reads=[("A_qTo", s)], writes=[("kT", mt)])

        NG_ = NMT * 4
        stage1(0)
        for g in range(NG_):
            if g + 1 < NG_:
                stage1(g + 1)
            stage2(g)
        P.barrier()
        P.flush()


def phase_B(P, nc, C, l):
    L = C.L
    NQT = L // 512
    NKT = L // 128
    lam_init = 0.8 - 0.6 * math.exp(-0.3 * l)
    ps = C.ps
    with ExitStack() as st:
        sb = lambda n, s, d: st.enter_context(nc.sbuf_tensor("%s_l%d" % (n, l), s, d))
        kT = [sb("B_kT%d" % i, [128, L], BF16) for i in range(2)]
        qT = [sb("B_qT%d" % i, [128, L], BF16) for i in range(2)]
        V = [sb("B_V%d" % i, [128, NKT, 128], BF16) for i in range(2)]
        Pt = [sb("B_P%d" % i, [128, 2, 512], BF16) for i in range(3)]
        r1 = sb("B_r1", [128, 512], F32)
        r2 = sb("B_r2", [128, 512], F32)
        o1 = sb("B_o1", [128, 512], F32)
        o2 = sb("B_o2", [128, 512], F32)
        oo = [sb("B_oo%d" % i, [128, 512], F32) for i in range(2)]
        sq = [sb("B_sq%d" % i, [128, 512], BF16) for i in range(2)]
        sd = sb("B_sd", [128, 512], F32)
        rs = sb("B_rs", [128, 512], F32)
        ot = [sb("B_ot%d" % i, [128, 512], BF16) for i in range(2)]
        lq = sb("B_lq", [128, 4, 64], F32)
        lp = sb("B_lp", [128, 2, 64], F32)
        le = sb("B_le", [128, 4], F32)
        gs = sb("B_gs", [128, 2], F32)

        P.dma("sp", lq[:], C.lam_qk[l:l + 1].rearrange("o a d -> o (a d)").partition_broadcast(128)
              .rearrange("p o (a d) -> p (o a) d", a=4), writes=[("B_lq",)])
        P.dma("sp", gs[:, 0:1], C.subln_g[l].rearrange("(p o) -> p o", o=1), writes=[("B_gs0",)])
        lqv = lq[:].rearrange("p (a b) d -> p a b d", b=2)
        P.op("dve", lambda e: e.tensor_tensor(out=lp[:], in0=lqv[:, :, 0, :], in1=lqv[:, :, 1, :], op=ALU.mult),
             reads=[("B_lq",)], writes=[("B_lp",)])
        P.op("dve", lambda e: e.reduce_sum(out=le[:, 0:2], in_=lp[:], axis=AX.X),
             reads=[("B_lp",)], writes=[("B_le0",)])
        P.op("act", lambda e: e.activation(out=le[:, 2:4], in_=le[:, 0:2], func=AF.Exp),
             reads=[("B_le0",)], writes=[("B_le1",)])
        P.op("dve", lambda e: e.tensor_tensor(out=le[:, 0:1], in0=le[:, 3:4], in1=le[:, 2:3], op=ALU.subtract),
             reads=[("B_le1",)], writes=[("B_le2",)])
        P.op("dve", lambda e: e.tensor_scalar(out=le[:, 1:2], in0=le[:, 0:1], scalar1=-lam_init, scalar2=None,
                                              op0=ALU.add),
             reads=[("B_le2",)], writes=[("B_neglam",)])
        P.op("dve", lambda e: e.tensor_scalar(out=gs[:, 1:2], in0=gs[:, 0:1], scalar1=(1.0 - lam_init), scalar2=None,
                                              op0=ALU.mult),
             reads=[("B_gs0",)], writes=[("B_gs",)])
        neglam = le[:, 1:2]
        gsc = gs[:, 1:2]

        def load_head(h):
            s = h % 2
            P.dma("sp", kT[s][:], C.kT[h], reads=[("kT", i) for i in range(NQT)], writes=[("B_kT", s)])
            P.dma("sp", qT[s][:], C.qT[h], reads=[("qT", i) for i in range(NQT)], writes=[("B_qT", s)])
            vsrc = C.v[:, h * 128:(h + 1) * 128].rearrange("(kt p) c -> p kt c", p=128)
            step = max(1, NKT // 4)
            for i in range(0, NKT, step):
                P.dma("sp", V[s][:, i:i + step, :], vsrc[:, i:i + step, :],
                      reads=[("v", j) for j in range(NQT)], writes=[("B_V", s)])

        load_head(0)
        it = 0
        ep = 0
        pending = []
        tri3 = C.tri_b[:].unsqueeze(1).broadcast_to([128, 2, 128])

        def emit_S(s, qt, kt, it_):
            b0 = 2 * (it_ % 2)
            off = max(0, kt - 4 * qt) * 128
            for m in range(2):
                P.op("pe", "matmul", out=ps[:, b0 + m, off:512],
                     lhsT=kT[s][m * 64:(m + 1) * 64, kt * 128:(kt + 1) * 128],
                     rhs=qT[s][m * 64:(m + 1) * 64, qt * 512 + off:(qt + 1) * 512], start=True, stop=True,
                     reads=[("B_kT", s), ("B_qT", s)], writes=[("ps", b0 + m)], sig=(m == 1))
            r = it_ % 3
            P.op("act", "activation", out=Pt[r][:, :, off:512], in_=ps[:, b0:b0 + 2, off:512], func=AF.Exp,
                 scale=0.125, reads=[("ps", b0), ("ps", b0 + 1)], writes=[("B_P", r)])
            if kt >= 4 * qt:
                P.op("pool", "tensor_tensor", out=Pt[r][:, :, off:off + 128], in0=Pt[r][:, :, off:off + 128],
                     in1=tri3, op=ALU.mult, reads=[("B_P", r), ("tri_b",)], writes=[("B_P", r)])

        def emit_PV(s, qt, kt, it_, nk):
            r = it_ % 3
            off = max(0, kt - 4 * qt) * 128
            last = (kt == nk - 1)
            for m in range(2):
                P.op("pe", "matmul", out=ps[:, 4 + m, off:512], lhsT=V[s][:, kt, :], rhs=Pt[r][:, m, off:512],
                     start=(kt == 0), stop=last,
                     reads=[("B_V", s), ("B_P", r)], writes=[("ps", 4 + m)], sig=False)
            for m in range(2):
                P.op("pe", "matmul", out=ps[:, 6 + m, off:512], lhsT=C.ones_b[:], rhs=Pt[r][:, m, off:512],
                     start=(kt == 0), stop=last,
                     reads=[("ones_b",), ("B_P", r)], writes=[("ps", 6 + m)], sig=(m == 1))

        def make_tail(e_, h, qt):
            def tail(bq):
                P.op("pe", "matmul", out=ps[:, bq, :], lhsT=C.ones_b[:], rhs=sq[e_][:], start=True, stop=True,
                     reads=[("ones_b",), ("B_sq", e_)], writes=[("ps", bq)])
                P.op("act", "activation", out=sd[:], in_=ps[:, bq, :], func=AF.Sqrt, scale=1.0 / 128.0,
                     bias=C.eps_col[:, 0:1], reads=[("ps", bq), ("eps_col",)], writes=[("B_sd",)])
                P.op("dve", "reciprocal", out=rs[:], in_=sd[:], reads=[("B_sd",)], writes=[("B_rs",)])
                P.op("dve", "scalar_tensor_tensor", out=ot[e_][:], in0=oo[e_][:], scalar=gsc, in1=rs[:],
                     op0=ALU.mult, op1=ALU.mult,
                     reads=[("B_oo", e_), ("B_rs",), ("B_gs",)], writes=[("B_ot", e_)])
                P.dma("sp", C.mixT[h * 128:(h + 1) * 128, qt * 512:(qt + 1) * 512], ot[e_][:],
                      reads=[("B_ot", e_)], writes=[("mixT", h, qt)])
            return tail

        for h in range(N_HEADS):
            s = h % 2
            if h + 1 < N_HEADS:
                load_head(h + 1)
            for qt in range(NQT):
                nk = 4 * (qt + 1)
                emit_S(s, qt, 0, it)
                for kt in range(nk):
                    if kt + 1 < nk:
                        emit_S(s, qt, kt + 1, it + kt + 1)
                    emit_PV(s, qt, kt, it + kt, nk)
                    if kt == 1 and pending:
                        for f in pending:
                            f(2 * ((it + kt + 2) % 2))
                        pending = []
                it += nk
                e_ = ep % 2
                ep += 1
                P.op("dve", "reciprocal", out=r1[:], in_=ps[:, 6, :], reads=[("ps", 6)], writes=[("B_r1",)])
                P.op("dve", "tensor_tensor", out=o1[:], in0=ps[:, 4, :], in1=r1[:], op=ALU.mult,
                     reads=[("ps", 4), ("B_r1",)], writes=[("B_o1",)])
                P.op("dve", "reciprocal", out=r2[:], in_=ps[:, 7, :], reads=[("ps", 7)], writes=[("B_r2",)])
                P.op("dve", "tensor_tensor", out=o2[:], in0=ps[:, 5, :], in1=r2[:], op=ALU.mult,
                     reads=[("ps", 5), ("B_r2",)], writes=[("B_o2",)])
                P.op("dve", "scalar_tensor_tensor", out=oo[e_][:], in0=o2[:], scalar=neglam, in1=o1[:],
                     op0=ALU.mult, op1=ALU.add,
                     reads=[("B_o1",), ("B_o2",), ("B_neglam",)], writes=[("B_oo", e_)])
                P.op("act", "activation", out=sq[e_][:], in_=oo[e_][:], func=AF.Square,
                     reads=[("B_oo", e_)], writes=[("B_sq", e_)])
                pending.append(make_tail(e_, h, qt))
        for f in pending:
            f(0)
        pending = []
        for f in pending:
            f()
        P.barrier()
        P.flush()


R1 = 16


def cmul_acc(P, dst_re, dst_im, src_re, src_im, wr, wi, wni, kd_re, kd_im, ks_re, ks_im, extra_reads=()):
    er = list(extra_reads)
    P.op("dve", "scalar_tensor_tensor", out=dst_re, in0=src_re, scalar=wr, in1=dst_re, op0=ALU.mult, op1=ALU.add,
         reads=[ks_re, kd_re] + er, writes=[kd_re])
    P.op("dve", "scalar_tensor_tensor", out=dst_im, in0=src_im, scalar=wr, in1=dst_im, op0=ALU.mult, op1=ALU.add,
         reads=[ks_im, kd_im] + er, writes=[kd_im])
    P.op("dve", "scalar_tensor_tensor", out=dst_re, in0=src_im, scalar=wni, in1=dst_re, op0=ALU.mult, op1=ALU.add,
         reads=[ks_im, kd_re] + er, writes=[kd_re])
    P.op("dve", "scalar_tensor_tensor", out=dst_im, in0=src_re, scalar=wi, in1=dst_im, op0=ALU.mult, op1=ALU.add,
         reads=[ks_re, kd_im] + er, writes=[kd_im])


def ssm_prep(P, nc, C, l, st, NS):
    ps = C.ps
    sb = lambda n, s, d: st.enter_context(nc.sbuf_tensor("%s_l%d" % (n, l), s, d))
    BT = sb("C_BT", [128, 32, 128], BF16)
    CT = sb("C_CT", [128, 32, 128], BF16)
    NPW = R1 + NS + 1
    PWR = sb("C_pwr", [128, NPW, 16], F32)
    PWI = sb("C_pwi", [128, NPW, 16], F32)
    PWN = sb("C_pwn", [128, NPW, 16], F32)
    Dcol = sb("C_D", [128, 4], F32)
    with ExitStack() as st2:
        sb2 = lambda n, s, d: st2.enter_context(nc.sbuf_tensor("%s_l%d" % (n, l), s, d))
        lre = sb2("C_lre", [128, 16], F32)
        lim = sb2("C_lim", [128, 16], F32)
        ldt = sb2("C_ldt", [128, 16], F32)
        tmp = [sb2("C_tmp%d" % i, [128, 16], F32) for i in range(12)]
        bre = sb2("C_bre", [128, 16, 16], F32)
        bim = sb2("C_bim", [128, 16, 16], F32)
        Bre = sb2("C_Bre", [128, 16, 16], F32)
        Bim = sb2("C_Bim", [128, 16, 16], F32)
        tb1 = sb2("C_tb1", [128, 16, 16], F32)
        WB = sb2("C_WB", [128, 16, 2, 128], F32)
        Cn = sb2("C_Cn", [16, 2, 2048], F32)
        halfpi = sb2("C_hpi", [128, 1], F32)
        slow = dict(allow_slow_non_contiguous=True)
        P.dma("sp", lre[:], C.ssm_lam_re[l].rearrange("(k g) p -> (g p) k", g=2), writes=[("C_lre",)], **slow)
        P.dma("sp", lim[:], C.ssm_lam_im[l].rearrange("(k g) p -> (g p) k", g=2), writes=[("C_lim",)], **slow)
        ldsrc = C.ssm_log_dt[l].rearrange("(k g) -> g k", g=2)
        for g2 in range(2):
            P.dma("sp", ldt[g2 * 64:(g2 + 1) * 64, :].unsqueeze(1), ldsrc[g2:g2 + 1, :].partition_broadcast(64),
                  writes=[("C_ldt", g2)], **slow)
        P.dma("sp", bre[:], C.ssm_b_re[l].rearrange("(k g) p c -> (g p) k c", g=2), writes=[("C_bre",)])
        P.dma("sp", bim[:], C.ssm_b_im[l].rearrange("(k g) p c -> (g p) k c", g=2), writes=[("C_bim",)])
        P.dma("sp", Cn[:, 0, :].rearrange("c (g p) -> c g p", p=64), C.ssm_c_re[l].rearrange("g c p -> c g p"),
              writes=[("C_Cn", 0)])
        P.dma("sp", Cn[:, 1, :].rearrange("c (g p) -> c g p", p=64), C.ssm_c_im[l].rearrange("g c p -> c g p"),
              writes=[("C_Cn", 1)])
        P.dma("sp", Dcol[:], C.ssm_d[l].rearrange("g c -> (g c)").rearrange("(m p) -> p m", p=128),
              writes=[("C_D",)], **slow)
        P.op("pool", "memset", ap=halfpi[:], constant=math.pi / 2, writes=[("C_hpi",)])
        P.op("pool", "memset", ap=WB[:], constant=0.0, writes=[("C_WB",)])
        P.op("pool", "memset", ap=CT[:], constant=0.0, writes=[("C_CT",)])

        cnt = [0]

        def V(name, out, in0, in1, op, rd, wr):
            P.op("dve", "tensor_tensor", out=out, in0=in0, in1=in1, op=op, reads=rd, writes=wr)

        def tk(i):
            return ("C_tmp", i)

        dt_, lrd, th, mag, cs_, sn_, ta, tb_, tc, are, aim, den = tmp
        P.op("act", "activation", out=dt_[:], in_=ldt[:], func=AF.Exp,
             reads=[("C_ldt", 0), ("C_ldt", 1)], writes=[tk(0)])
        V("lrd", lrd[:], lre[:], dt_[:], ALU.mult, [("C_lre",), tk(0)], [tk(1)])
        V("th", th[:], lim[:], dt_[:], ALU.mult, [("C_lim",), tk(0)], [tk(2)])
        P.op("act", "activation", out=mag[:], in_=lrd[:], func=AF.Exp, reads=[tk(1)], writes=[tk(3)])
        P.op("act", "activation", out=sn_[:], in_=th[:], func=AF.Sin, scale=0.125, reads=[tk(2)], writes=[tk(5)])
        P.op("act", "activation", out=cs_[:], in_=th[:], func=AF.Sin, scale=-0.125, bias=halfpi[:, 0:1],
             reads=[tk(2), ("C_hpi",)], writes=[tk(4)])
        for _ in range(3):
            V("", ta[:], cs_[:], cs_[:], ALU.mult, [tk(4)], [tk(6)])
            V("", tb_[:], sn_[:], sn_[:], ALU.mult, [tk(5)], [tk(7)])
            V("", tc[:], cs_[:], sn_[:], ALU.mult, [tk(4), tk(5)], [tk(8)])
            V("", cs_[:], ta[:], tb_[:], ALU.subtract, [tk(6), tk(7)], [tk(4)])
            V("", sn_[:], tc[:], tc[:], ALU.add, [tk(8)], [tk(5)])
        V("", are[:], mag[:], cs_[:], ALU.mult, [tk(3), tk(4)], [tk(9)])
        V("", aim[:], mag[:], sn_[:], ALU.mult, [tk(3), tk(5)], [tk(10)])
        kp = lambda i: ("C_pw", i)

        def pw_set(i, re_ap, im_ap, rd):
            P.op("dve", "tensor_copy", out=PWR[:, i, :], in_=re_ap, reads=rd, writes=[("C_pwr", i)])
            P.op("dve", "tensor_copy", out=PWI[:, i, :], in_=im_ap, reads=rd, writes=[("C_pwi", i)])
            P.op("dve", "tensor_scalar", out=PWN[:, i, :], in0=im_ap, scalar1=-1.0, scalar2=None, op0=ALU.mult,
                 reads=rd, writes=[("C_pwn", i)])

        def pw_mul(i, j, b):
            rd = [("C_pwr", j), ("C_pwi", j), ("C_pwr", b), ("C_pwi", b), ("C_pwn", b)]
            V("", ta[:], PWR[:, j, :], PWR[:, b, :], ALU.mult, rd, [tk(6)])
            V("", tb_[:], PWI[:, j, :], PWN[:, b, :], ALU.mult, rd, [tk(7)])
            V("", PWR[:, i, :], ta[:], tb_[:], ALU.add, [tk(6), tk(7)], [("C_pwr", i)])
            V("", ta[:], PWR[:, j, :], PWI[:, b, :], ALU.mult, rd, [tk(6)])
            V("", tb_[:], PWI[:, j, :], PWR[:, b, :], ALU.mult, rd, [tk(7)])
            V("", PWI[:, i, :], ta[:], tb_[:], ALU.add, [tk(6), tk(7)], [("C_pwi", i)])
            P.op("dve", "tensor_scalar", out=PWN[:, i, :], in0=PWI[:, i, :], scalar1=-1.0, scalar2=None,
                 op0=ALU.mult, reads=[("C_pwi", i)], writes=[("C_pwn", i)])

        pw_set(1, are[:], aim[:], [tk(9), tk(10)])
        for n in range(2, R1 + 1):
            pw_mul(n, n - 1, 1)
        for s_ in range(1, NS):
            pw_mul(R1 + s_, R1 + s_ - 1, R1 + s_ - 1)
        nr, fr, fi = ta, tb_, tc
        P.op("dve", "tensor_scalar", out=nr[:], in0=are[:], scalar1=-1.0, scalar2=None, op0=ALU.add,
             reads=[tk(9)], writes=[tk(6)])
        V("", den[:], lre[:], lre[:], ALU.mult, [("C_lre",)], [tk(11)])
        V("", dt_[:], lim[:], lim[:], ALU.mult, [("C_lim",)], [tk(0)])
        V("", den[:], den[:], dt_[:], ALU.add, [tk(11), tk(0)], [tk(11)])
        P.op("dve", "reciprocal", out=den[:], in_=den[:], reads=[tk(11)], writes=[tk(11)])
        V("", fr[:], nr[:], lre[:], ALU.mult, [tk(6), ("C_lre",)], [tk(7)])
        V("", dt_[:], aim[:], lim[:], ALU.mult, [tk(10), ("C_lim",)], [tk(0)])
        V("", fr[:], fr[:], dt_[:], ALU.add, [tk(7), tk(0)], [tk(7)])
        V("", fr[:], fr[:], den[:], ALU.mult, [tk(7), tk(11)], [tk(7)])
        V("", fi[:], aim[:], lre[:], ALU.mult, [tk(10), ("C_lre",)], [tk(8)])
        V("", dt_[:], nr[:], lim[:], ALU.mult, [tk(6), ("C_lim",)], [tk(0)])
        V("", fi[:], fi[:], dt_[:], ALU.subtract, [tk(8), tk(0)], [tk(8)])
        V("", fi[:], fi[:], den[:], ALU.mult, [tk(8), tk(11)], [tk(8)])
        frb = fr[:].unsqueeze(2).broadcast_to([128, 16, 16])
        fib = fi[:].unsqueeze(2).broadcast_to([128, 16, 16])
        V("", Bre[:], bre[:], frb, ALU.mult, [("C_bre",), tk(7)], [("C_Bre",)])
        V("", tb1[:], bim[:], fib, ALU.mult, [("C_bim",), tk(8)], [("C_tb1",)])
        V("", Bre[:], Bre[:], tb1[:], ALU.subtract, [("C_Bre",), ("C_tb1",)], [("C_Bre",)])
        V("", Bim[:], bim[:], frb, ALU.mult, [("C_bim",), tk(7)], [("C_Bim",)])
        V("", tb1[:], bre[:], fib, ALU.mult, [("C_bre",), tk(8)], [("C_tb1",)])
        V("", Bim[:], Bim[:], tb1[:], ALU.add, [("C_Bim",), ("C_tb1",)], [("C_Bim",)])
        for g2 in range(2):
            for q4 in range(4):
                c0 = q4 * 32 + g2 * 16
                for ri, src in ((0, Bre), (1, Bim)):
                    P.op("dve", "tensor_copy", out=WB[g2 * 64:(g2 + 1) * 64, q4::4, ri, c0:c0 + 16],
                         in_=src[g2 * 64:(g2 + 1) * 64, q4::4, :],
                         reads=[("C_Bre",), ("C_Bim",), ("C_WB",)], writes=[("C_WBf", g2, q4, ri)])
        wbf = [("C_WBf", g2, q4, ri) for g2 in range(2) for q4 in range(4) for ri in range(2)]
        for grp in range(8):
            b = grp % 2
            for j in range(4):
                idx = grp * 4 + j
                P.op("pe", "transpose", out=ps[:, b, j * 128:(j + 1) * 128], in_=WB[:, idx // 2, idx % 2, :],
                     identity=C.ident_f[:], reads=wbf + [("ident_f",)], writes=[("ps", b)], sig=(j == 3))
            P.op("act", "activation", out=BT[:, grp * 4:(grp + 1) * 4, :],
                 in_=ps[:, b, :].rearrange("p (j c) -> p j c", j=4), func=AF.Copy,
                 reads=[("ps", b)], writes=[("C_BT", grp)])
        for k in range(16):
            for ri in range(2):
                P.op("pe", "transpose", out=ps[:, 2, (k * 2 + ri) * 16:(k * 2 + ri + 1) * 16],
                     in_=Cn[0:16, ri, k * 128:(k + 1) * 128], identity=C.ident_f[0:16, 0:16],
                     reads=[("C_Cn", ri), ("ident_f",)], writes=[("ps", 2)], sig=(k == 15 and ri == 1))
        psC = ps[:, 2, :].rearrange("p (k r c) -> p k r c", k=16, r=2)
        CTv = CT[:].rearrange("p (k r) c -> p k r c", r=2)
        for g2 in range(2):
            for q4 in range(4):
                c0 = q4 * 32 + g2 * 16
                for ri in range(2):
                    P.op("act", "activation", out=CTv[g2 * 64:(g2 + 1) * 64, q4::4, ri, c0:c0 + 16],
                         in_=psC[g2 * 64:(g2 + 1) * 64, q4::4, ri, :], func=AF.Copy,
                         scale=(1.0 if ri == 0 else -1.0),
                         reads=[("ps", 2), ("C_CT",)], writes=[("C_CTf", g2, q4, ri)])
        P.barrier()
        P.flush()
    ctf = [("C_CTf", g2, q4, ri) for g2 in range(2) for q4 in range(4) for ri in range(2)]

    return BT, CT, PWR, PWI, PWN, Dcol, ctf


def phase_C(P, nc, C, l):
    L = C.L
    TS = L
    J1 = TS // R1
    NS = J1.bit_length() - 1
    assert (1 << NS) == J1
    NB = TS // 512
    ps = C.ps
    with ExitStack() as st:
        sb = lambda n, s, d: st.enter_context(nc.sbuf_tensor("%s_l%d" % (n, l), s, d))
        BT, CT, PWR, PWI, PWN, Dcol, ctf = ssm_prep(P, nc, C, l, st, NS)
        assert J1 <= 512
        X = sb("C_X", [128, 2, R1, J1], F32)
        Xb = sb("C_Xb", [128, 2, R1, J1], BF16)
        uTt = sb("C_uTt", [128, TS], BF16)
        yraw = sb("C_yraw", [128, R1, J1], F32)
        ta_ = [sb("C_ga%d" % i, [128, 512], F32) for i in range(2)]
        tb_2 = [sb("C_gb%d" % i, [128, 512], F32) for i in range(2)]
        go = [sb("C_go%d" % i, [128, 512], BF16) for i in range(2)]
        HS = [sb("C_hs%d" % i, [128, 2, J1], F32) for i in range(2)]
        bu_rr = 0
        y_rr = 0
        g_rr = 0
        NQ = L // 512
        RG = 4
        kx = lambda c, r: ("C_Xc", c, r)
        for m in range(4):
            P.dma("sp", uTt[:], C.uT[m * 128:(m + 1) * 128, :], reads=[("uT", i) for i in range(NQ)],
                  writes=[("C_uTt",)])
            for q4 in range(4):
                k = m * 4 + q4
                sc = lambda T_, i, k=k: T_[:, i, k:k + 1]
                for r in range(R1):
                    b0 = 2 * (bu_rr % 2)
                    bu_rr += 1
                    for ri in range(2):
                        P.op("pe", "matmul", out=ps[:, b0 + ri, 0:J1], lhsT=BT[:, k * 2 + ri, :],
                             rhs=uTt[:, r::R1], start=True, stop=True,
                             reads=[("C_BT", (k * 2 + ri) // 4), ("C_uTt",)], writes=[("ps", b0 + ri)], sig=(ri == 1))
                    P.op("act", "activation", out=X[:, :, r, :], in_=ps[:, b0:b0 + 2, 0:J1],
                         func=AF.Copy, reads=[("ps", b0), ("ps", b0 + 1)], writes=[kx(0, r), kx(1, r)])
                for r in range(1, R1):
                    cmul_acc(P, X[:, 0, r, :], X[:, 1, r, :], X[:, 0, r - 1, :], X[:, 1, r - 1, :],
                             sc(PWR, 1), sc(PWI, 1), sc(PWN, 1), kx(0, r), kx(1, r), kx(0, r - 1), kx(1, r - 1))
                klast = (kx(0, R1 - 1), kx(1, R1 - 1))
                src = (X[:, 0, R1 - 1, :], X[:, 1, R1 - 1, :])
                srck = klast
                Xtop = X[:, :, R1 - 1, :]
                src2 = Xtop
                for s_ in range(NS):
                    d = 1 << s_
                    if s_ == NS - 1:
                        dst = (X[:, 0, R1 - 1, :], X[:, 1, R1 - 1, :])
                        dstk = klast
                        dst2 = Xtop
                    else:
                        hb = HS[s_ % 2]
                        dst = (hb[:, 0, :], hb[:, 1, :])
                        dstk = (("C_hs", s_ % 2, 0), ("C_hs", s_ % 2, 1))
                        dst2 = hb[:]
                    wr, wi, wni = sc(PWR, R1 + s_), sc(PWI, R1 + s_), sc(PWN, R1 + s_)
                    if dst2 is not src2:
                        P.op("dve", "tensor_copy", out=dst2[:, :, 0:d], in_=src2[:, :, 0:d],
                             reads=list(srck), writes=list(dstk))
                    n = J1 - d
                    P.op("dve", "scalar_tensor_tensor", out=dst[0][:, d:J1], in0=src[0][:, 0:n], scalar=wr,
                         in1=src[0][:, d:J1], op0=ALU.mult, op1=ALU.add, reads=list(srck), writes=[dstk[0]])
                    P.op("dve", "scalar_tensor_tensor", out=dst[1][:, d:J1], in0=src[1][:, 0:n], scalar=wr,
                         in1=src[1][:, d:J1], op0=ALU.mult, op1=ALU.add, reads=list(srck), writes=[dstk[1]])
                    P.op("dve", "scalar_tensor_tensor", out=dst[0][:, d:J1], in0=src[1][:, 0:n], scalar=wni,
                         in1=dst[0][:, d:J1], op0=ALU.mult, op1=ALU.add, reads=list(srck) + [dstk[0]],
                         writes=[dstk[0]])
                    P.op("dve", "scalar_tensor_tensor", out=dst[1][:, d:J1], in0=src[0][:, 0:n], scalar=wi,
                         in1=dst[1][:, d:J1], op0=ALU.mult, op1=ALU.add, reads=list(srck) + [dstk[1]],
                         writes=[dstk[1]])
                    src, srck, src2 = dst, dstk, dst2
                for r in range(R1 - 1):
                    cmul_acc(P, X[:, 0, r, 1:J1], X[:, 1, r, 1:J1],
                             X[:, 0, R1 - 1, 0:J1 - 1], X[:, 1, R1 - 1, 0:J1 - 1],
                             sc(PWR, r + 1), sc(PWI, r + 1), sc(PWN, r + 1),
                             kx(0, r), kx(1, r), klast[0], klast[1])
                for rg in range(R1 // RG):
                    rs_ = slice(rg * RG, (rg + 1) * RG)
                    P.op("act", "activation", out=Xb[:, :, rs_, :], in_=X[:, :, rs_, :], func=AF.Copy,
                         reads=[kx(c, r) for c in range(2) for r in range(rg * RG, (rg + 1) * RG)] + list(klast),
                         writes=[("C_Xb", rg)])
                for r in range(R1):
                    b = 4 + y_rr % 4
                    y_rr += 1
                    for ri in range(2):
                        P.op("pe", "matmul", out=ps[:, b, 0:J1], lhsT=CT[:, k * 2 + ri, :],
                             rhs=Xb[:, ri, r, :], start=(ri == 0), stop=(ri == 1),
                             reads=ctf + [("C_Xb", r // RG)], writes=[("ps", b)], sig=(ri == 1))
                    P.op("act", "activation", out=yraw[q4 * 32:(q4 + 1) * 32, r, :],
                         in_=ps[q4 * 32:(q4 + 1) * 32, b, 0:J1], func=AF.Copy,
                         reads=[("ps", b)], writes=[("C_yraw", q4, r)])
            for r in range(R1):
                i = g_rr % 2
                g_rr += 1
                P.op("dve", "scalar_tensor_tensor", out=ta_[i][:, 0:J1], in0=uTt[:, r::R1], scalar=Dcol[:, m:m + 1],
                     in1=yraw[:, r, :], op0=ALU.mult, op1=ALU.add,
                     reads=[("C_uTt",), ("C_D",)] + [("C_yraw", q, r) for q in range(4)], writes=[("C_ga", i)])
                P.op("pool", "tensor_tensor", out=tb_2[i][:, 0:J1], in0=ta_[i][:, 0:J1], in1=ta_[i][:, 0:J1],
                     op=ALU.mult, reads=[("C_ga", i)], writes=[("C_gb", i)])
                P.op("pool", "tensor_scalar", out=tb_2[i][:, 0:J1], in0=tb_2[i][:, 0:J1], scalar1=0.044715,
                     scalar2=1.0, op0=ALU.mult, op1=ALU.add, reads=[("C_gb", i)], writes=[("C_gb", i)])
                P.op("pool", "tensor_tensor", out=tb_2[i][:, 0:J1], in0=tb_2[i][:, 0:J1], in1=ta_[i][:, 0:J1],
                     op=ALU.mult, reads=[("C_gb", i), ("C_ga", i)], writes=[("C_gb", i)])
                P.op("act", "activation", out=tb_2[i][:, 0:J1], in_=tb_2[i][:, 0:J1], func=AF.Sigmoid,
                     scale=2.0 * math.sqrt(2.0 / math.pi), reads=[("C_gb", i)], writes=[("C_gb", i)])
                P.op("pool", "tensor_tensor", out=go[i][:, 0:J1], in0=ta_[i][:, 0:J1], in1=tb_2[i][:, 0:J1],
                     op=ALU.mult, reads=[("C_ga", i), ("C_gb", i)], writes=[("C_go", i)])
                P.dma("sp", C.gT[m * 128:(m + 1) * 128, r * J1:(r + 1) * J1], go[i][:, 0:J1],
                      reads=[("C_go", i)], writes=[("gT", m, r)])
        P.barrier()
        P.flush()
    with ExitStack() as st:
        sb = lambda n, s, d: st.enter_context(nc.sbuf_tensor("%s_l%d" % (n, l), s, d))
        GW = sb("C_GW", [128, 4, 512], BF16)
        gb = sb("C_gbias", [128, 4], F32)
        gt = [sb("C_gt%d" % i, [128, 4, J1], BF16) for i in range(2)]
        sg = [sb("C_sg%d" % i, [128, 512], F32) for i in range(2)]
        SO = sb("C_SO", [128, 4, L], BF16)
        P.dma("pool", GW[:], C.glu_w[l].rearrange("(kt p) n -> p kt n", p=128), writes=[("C_GW",)])
        P.dma("sp", gb[:], C.glu_b[l].rearrange("(m p) -> p m", p=128), writes=[("C_gbias",)],
              allow_slow_non_contiguous=True)
        rr = 0
        for r in range(R1):
            i = r % 2
            P.dma("sp", gt[i][:], C.gT[:, r * J1:(r + 1) * J1].rearrange("(m p) t -> p m t", p=128),
                  reads=[("gT", m, r) for m in range(4)], writes=[("C_gt", i)])
            for mo in range(4):
                b = rr % 4
                rr += 1
                for kt in range(4):
                    P.op("pe", "matmul", out=ps[:, b, 0:J1], lhsT=GW[:, kt, mo * 128:(mo + 1) * 128],
                         rhs=gt[i][:, kt, :], start=(kt == 0), stop=(kt == 3),
                         reads=[("C_GW",), ("C_gt", i)], writes=[("ps", b)], sig=(kt == 3))
                j = rr % 2
                P.op("act", "activation", out=sg[j][:, 0:J1], in_=ps[:, b, 0:J1], func=AF.Sigmoid,
                     bias=gb[:, mo:mo + 1], reads=[("ps", b), ("C_gbias",)], writes=[("C_sg", j)])
                P.op("dve", "tensor_tensor", out=SO[:, mo, r::R1], in0=gt[i][:, mo, :], in1=sg[j][:, 0:J1],
                     op=ALU.mult, reads=[("C_gt", i), ("C_sg", j)], writes=[("C_SO", mo, r)])
        for mo in range(4):
            for hf in range(2):
                sl = slice(hf * (L // 2), (hf + 1) * (L // 2))
                P.dma("sp", C.mixT[512 + mo * 128:512 + (mo + 1) * 128, sl], SO[:, mo, sl],
                      reads=[("C_SO", mo, r) for r in range(R1)], writes=[("mixT", 4, mo, hf)])
        P.barrier()
        P.flush()


class Rot:
    def __init__(self):
        self.i = 0

    def next(self):
        b = 2 * (self.i % 2)
        self.i += 1
        return b


def phase_BC(P, nc, C, l):
    from collections import deque
    L = C.L
    NQT = L // 512
    NKT = L // 128
    import os
    TS = int(os.environ.get("BC_TS", min(L, 4096)))
    NH = L // TS
    J1 = TS // R1
    NS = J1.bit_length() - 1
    assert (1 << NS) == J1 and J1 <= 512
    RG = 4
    lam_init = 0.8 - 0.6 * math.exp(-0.3 * l)
    ps = C.ps
    with ExitStack() as st:
        sb = lambda n, s, d: st.enter_context(nc.sbuf_tensor("%s_l%d" % (n, l), s, d))
        BT, CT, PWR, PWI, PWN, Dcol, ctf = ssm_prep(P, nc, C, l, st, NS)
        kT = sb("B_kT", [128, L], BF16)
        V = sb("B_V", [128, NKT, 128], BF16)
        qTt = [sb("B_qT%d" % i, [128, 512], BF16) for i in range(2)]
        Pt = [sb("B_P%d" % i, [128, 2, 512], BF16) for i in range(3)]
        r1 = sb("B_r1", [128, 512], F32)
        r2 = sb("B_r2", [128, 512], F32)
        o1 = sb("B_o1", [128, 512], F32)
        o2 = sb("B_o2", [128, 512], F32)
        oo = [sb("B_oo%d" % i, [128, 512], F32) for i in range(2)]
        sq = [sb("B_sq%d" % i, [128, 512], BF16) for i in range(2)]
        sd = sb("B_sd", [128, 512], F32)
        rs = sb("B_rs", [128, 512], F32)
        ot = [sb("B_ot%d" % i, [128, 512], BF16) for i in range(2)]
        lq = sb("B_lq", [128, 4, 64], F32)
        lp = sb("B_lp", [128, 2, 64], F32)
        le = sb("B_le", [128, 4], F32)
        gs = sb("B_gs", [128, 2], F32)
        X = [sb("C_X%d" % i, [128, 2, R1, J1], F32) for i in range(2)]
        Xb = [sb("C_Xb%d" % i, [128, 2, RG, J1], BF16) for i in range(2)]
        uTt = [sb("C_uTt%d" % i, [128, TS], BF16) for i in range(2)]
        yraw = sb("C_yraw", [128, R1, J1], F32)
        carry = sb("C_carry", [128, 16, 2], F32)
        ta_ = [sb("C_ga%d" % i, [128, J1], F32) for i in range(2)]
        tb_2 = [sb("C_gb%d" % i, [128, J1], F32) for i in range(2)]
        go = [sb("C_go%d" % i, [128, J1], BF16) for i in range(2)]
        HS = [sb("C_hs%d" % i, [128, 2, J1], F32) for i in range(2)]

        P.dma("sp", lq[:], C.lam_qk[l:l + 1].rearrange("o a d -> o (a d)").partition_broadcast(128)
              .rearrange("p o (a d) -> p (o a) d", a=4), writes=[("B_lq",)])
        P.dma("sp", gs[:, 0:1], C.subln_g[l].rearrange("(p o) -> p o", o=1), writes=[("B_gs0",)])
        lqv = lq[:].rearrange("p (a b) d -> p a b d", b=2)
        P.op("dve", "tensor_tensor", out=lp[:], in0=lqv[:, :, 0, :], in1=lqv[:, :, 1, :], op=ALU.mult,
             reads=[("B_lq",)], writes=[("B_lp",)])
        P.op("dve", "reduce_sum", out=le[:, 0:2], in_=lp[:], axis=AX.X, reads=[("B_lp",)], writes=[("B_le0",)])
        P.op("act", "activation", out=le[:, 2:4], in_=le[:, 0:2], func=AF.Exp, reads=[("B_le0",)], writes=[("B_le1",)])
        P.op("dve", "tensor_tensor", out=le[:, 0:1], in0=le[:, 3:4], in1=le[:, 2:3], op=ALU.subtract,
             reads=[("B_le1",)], writes=[("B_le2",)])
        P.op("dve", "tensor_scalar", out=le[:, 1:2], in0=le[:, 0:1], scalar1=-lam_init, scalar2=None, op0=ALU.add,
             reads=[("B_le2",)], writes=[("B_neglam",)])
        P.op("dve", "tensor_scalar", out=gs[:, 1:2], in0=gs[:, 0:1], scalar1=(1.0 - lam_init), scalar2=None,
             op0=ALU.mult, reads=[("B_gs0",)], writes=[("B_gs",)])
        neglam = le[:, 1:2]
        gsc = gs[:, 1:2]
        tri3 = C.tri_b[:].unsqueeze(1).broadcast_to([128, 2, 128])
        rot = Rot()

        pslots = deque()
        bst = dict(pt=0, ep=0)
        pending = []

        NCH = min(8, NKT)
        KPC = NKT // NCH

        def load_head(h):
            vsrc = C.v[:, h * 128:(h + 1) * 128].rearrange("(kt p) c -> p kt c", p=128)
            for c_ in range(NCH):
                k0, k1 = c_ * KPC, (c_ + 1) * KPC
                P.dma("sp", kT[:, k0 * 128:k1 * 128], C.kT[h, :, k0 * 128:k1 * 128], writes=[("B_kT", c_)])
                P.dma("sp", V[:, k0:k1, :], vsrc[:, k0:k1, :], writes=[("B_V", c_)])

        def load_q(h, qt):
            s2 = (h * NQT + qt) % 2
            P.dma("sp", qTt[s2][:], C.qT[h, :, qt * 512:(qt + 1) * 512], writes=[("B_qT", s2)])

        def emit_S(h, qt, kt):
            s2 = (h * NQT + qt) % 2
            b0 = rot.next()
            off = max(0, kt - 4 * qt) * 128
            for m in range(2):
                P.op("pe", "matmul", out=ps[:, b0 + m, off:512],
                     lhsT=kT[m * 64:(m + 1) * 64, kt * 128:(kt + 1) * 128],
                     rhs=qTt[s2][m * 64:(m + 1) * 64, off:512], start=True, stop=True,
                     reads=[("B_kT", kt // KPC), ("B_qT", s2)], writes=[("ps", b0 + m)], sig=(m == 1))
            r = bst["pt"] % 3
            bst["pt"] += 1
            pslots.append(r)
            P.op("act", "activation", out=Pt[r][:, :, off:512], in_=ps[:, b0:b0 + 2, off:512], func=AF.Exp,
                 scale=0.125, reads=[("ps", b0), ("ps", b0 + 1)], writes=[("B_P", r)])
            if kt >= 4 * qt:
                P.op("pool", "tensor_tensor", out=Pt[r][:, :, off:off + 128], in0=Pt[r][:, :, off:off + 128],
                     in1=tri3, op=ALU.mult, reads=[("B_P", r), ("tri_b",)], writes=[("B_P", r)])

        def emit_PV(h, qt, kt, nk):
            r = pslots.popleft()
            off = max(0, kt - 4 * qt) * 128
            last = (kt == nk - 1)
            for m in range(2):
                P.op("pe", "matmul", out=ps[:, 4 + m, off:512], lhsT=V[:, kt, :], rhs=Pt[r][:, m, off:512],
                     start=(kt == 0), stop=last, reads=[("B_V", kt // KPC), ("B_P", r)], writes=[("ps", 4 + m)],
                     sig=False)
            for m in range(2):
                P.op("pe", "matmul", out=ps[:, 6 + m, off:512], lhsT=C.ones_b[:], rhs=Pt[r][:, m, off:512],
                     start=(kt == 0), stop=last, reads=[("ones_b",), ("B_P", r)], writes=[("ps", 6 + m)],
                     sig=(m == 1))

        def make_tail(e_, h, qt):
            def tail():
                bq = rot.next()
                P.op("pe", "matmul", out=ps[:, bq, :], lhsT=C.ones_b[:], rhs=sq[e_][:], start=True, stop=True,
                     reads=[("ones_b",), ("B_sq", e_)], writes=[("ps", bq)])
                P.op("act", "activation", out=sd[:], in_=ps[:, bq, :], func=AF.Ln, scale=1.0 / 128.0,
                     bias=C.eps_col[:, 0:1], reads=[("ps", bq), ("eps_col",)], writes=[("B_sd",)])
                P.op("act", "activation", out=rs[:], in_=sd[:], func=AF.Exp, scale=-0.5,
                     reads=[("B_sd",)], writes=[("B_rs",)])
                P.op("dve", "scalar_tensor_tensor", out=ot[e_][:], in0=oo[e_][:], scalar=gsc, in1=rs[:],
                     op0=ALU.mult, op1=ALU.mult, reads=[("B_oo", e_), ("B_rs",), ("B_gs",)], writes=[("B_ot", e_)])
                P.dma("sp", C.mixT[h * 128:(h + 1) * 128, qt * 512:(qt + 1) * 512], ot[e_][:],
                      reads=[("B_ot", e_)], writes=[("mixT", h, qt)])
            return tail

        def epilogue(h, qt):
            e_ = bst["ep"] % 2
            bst["ep"] += 1
            P.op("act", "activation", out=r1[:], in_=ps[:, 6, :], func=AF.Copy, reads=[("ps", 6)], writes=[("B_r1",)])
            P.op("act", "activation", out=r2[:], in_=ps[:, 7, :], func=AF.Copy, reads=[("ps", 7)], writes=[("B_r2",)])
            P.op("dve", "tensor_copy", out=o1[:], in_=ps[:, 4, :], reads=[("ps", 4)], writes=[("B_o1",)])
            P.op("dve", "tensor_copy", out=o2[:], in_=ps[:, 5, :], reads=[("ps", 5)], writes=[("B_o2",)])
            P.op("dve", "reciprocal", out=r1[:], in_=r1[:], reads=[("B_r1",)], writes=[("B_r1",)])
            P.op("dve", "tensor_tensor", out=o1[:], in0=o1[:], in1=r1[:], op=ALU.mult,
                 reads=[("B_o1",), ("B_r1",)], writes=[("B_o1",)])
            P.op("dve", "reciprocal", out=r2[:], in_=r2[:], reads=[("B_r2",)], writes=[("B_r2",)])
            P.op("dve", "tensor_tensor", out=o2[:], in0=o2[:], in1=r2[:], op=ALU.mult,
                 reads=[("B_o2",), ("B_r2",)], writes=[("B_o2",)])
            P.op("dve", "scalar_tensor_tensor", out=oo[e_][:], in0=o2[:], scalar=neglam, in1=o1[:],
                 op0=ALU.mult, op1=ALU.add, reads=[("B_o1",), ("B_o2",), ("B_neglam",)], writes=[("B_oo", e_)])
            P.op("dve", "tensor_tensor", out=sq[e_][:], in0=oo[e_][:], in1=oo[e_][:], op=ALU.mult,
                 reads=[("B_oo", e_)], writes=[("B_sq", e_)])
            pending.append(make_tail(e_, h, qt))

        def b_iter(h, qt, kt, nk):
            def f():
                if kt + 1 < nk:
                    emit_S(h, qt, kt + 1)
                emit_PV(h, qt, kt, nk)
                if kt == min(7, nk - 1):
                    while pending:
                        pending.pop(0)()
            return f

        def b_start(h, qt):
            def f():
                if qt == 0:
                    load_head(h)
                if h == 0 and qt == 0:
                    load_q(0, 0)
                nidx = h * NQT + qt + 1
                if nidx < N_HEADS * NQT:
                    load_q(nidx // NQT, nidx % NQT)
                emit_S(h, qt, 0)
            return f

        b_items = []
        for h in range(N_HEADS):
            for qt in range(NQT):
                nk = 4 * (qt + 1)
                b_items.append((0.3, b_start(h, qt)))
                for kt in range(nk):
                    b_items.append((1.0, b_iter(h, qt, kt, nk)))
                b_items.append((2.0, (lambda h=h, qt=qt: epilogue(h, qt))))

        units = [(m, hf, q4) for m in range(4) for hf in range(NH) for q4 in range(4)]
        NU = len(units)
        kx = lambda xs, c, r: ("C_Xc", xs, c, r)
        s1banks = {}
        s4banks = {}

        def load_u(g4):
            m, hf = g4 // NH, g4 % NH
            P.dma("sp", uTt[g4 % 2][:], C.uT[m * 128:(m + 1) * 128, hf * TS:(hf + 1) * TS],
                  writes=[("C_uTt", g4 % 2)])

        RP = max(1, min(RG, 512 // J1))

        def S1_pe(u, r):
            m, hf, q4 = units[u]
            k = m * 4 + q4
            us = (u // 4) % 2
            b0 = rot.next()
            s1banks[(u, r)] = b0
            uv = uTt[us][:].rearrange("p (j r) -> p r j", r=R1)
            for ri in range(2):
                P.op("pe", "matmul", out=ps[:, b0 + ri, 0:RP * J1].rearrange("p (r j) -> p r j", r=RP),
                     lhsT=BT[:, k * 2 + ri, :], rhs=uv[:, r:r + RP, :],
                     start=True, stop=True, reads=[("C_BT", (k * 2 + ri) // 4), ("C_uTt", us)],
                     writes=[("ps", b0 + ri)], sig=(ri == 1))

        def S1_act(u, r):
            xs = u % 2
            b0 = s1banks.pop((u, r))
            P.op("act", "activation", out=X[xs][:, :, r:r + RP, :],
                 in_=ps[:, b0:b0 + 2, 0:RP * J1].rearrange("p c (r j) -> p c r j", r=RP), func=AF.Copy,
                 reads=[("ps", b0), ("ps", b0 + 1)],
                 writes=[kx(xs, c, rr) for c in range(2) for rr in range(r, r + RP)])

        def S2_gen(u):
            m, hf, q4 = units[u]
            k = m * 4 + q4
            xs = u % 2
            Xs = X[xs]
            sc = lambda T_, i: T_[:, i, k:k + 1]
            ops = []

            def stt(out, in0, scalar, in1, rd, wr):
                ops.append(lambda: P.op("dve", "scalar_tensor_tensor", out=out, in0=in0, scalar=scalar, in1=in1,
                                        op0=ALU.mult, op1=ALU.add, reads=rd, writes=wr))

            def cacc(dre, dim, sre, sim, pi, kdr, kdi, ksr, ksi, extra=()):
                wr, wi, wni = sc(PWR, pi), sc(PWI, pi), sc(PWN, pi)
                ex = list(extra)
                stt(dre, sre, wr, dre, [ksr, kdr] + ex, [kdr])
                stt(dim, sim, wr, dim, [ksi, kdi] + ex, [kdi])
                stt(dre, sim, wni, dre, [ksi, kdr] + ex, [kdr])
                stt(dim, sre, wi, dim, [ksr, kdi] + ex, [kdi])

            kc = ("C_carry", k)
            if hf > 0:
                cacc(Xs[:, 0, 0, 0:1], Xs[:, 1, 0, 0:1], carry[:, k, 0:1], carry[:, k, 1:2], 1,
                     kx(xs, 0, 0), kx(xs, 1, 0), kc, kc)
            for r in range(1, R1):
                cacc(Xs[:, 0, r, :], Xs[:, 1, r, :], Xs[:, 0, r - 1, :], Xs[:, 1, r - 1, :], 1,
                     kx(xs, 0, r), kx(xs, 1, r), kx(xs, 0, r - 1), kx(xs, 1, r - 1))
            klast = (kx(xs, 0, R1 - 1), kx(xs, 1, R1 - 1))
            src = (Xs[:, 0, R1 - 1, :], Xs[:, 1, R1 - 1, :])
            srck = klast
            Xtop = Xs[:, :, R1 - 1, :]
            src2 = Xtop
            for s_ in range(NS):
                d = 1 << s_
                if s_ == NS - 1:
                    dst, dstk, dst2 = (Xs[:, 0, R1 - 1, :], Xs[:, 1, R1 - 1, :]), klast, Xtop
                else:
                    hb = HS[s_ % 2]
                    dst, dstk, dst2 = (hb[:, 0, :], hb[:, 1, :]), (("C_hs", s_ % 2, 0), ("C_hs", s_ % 2, 1)), hb[:]
                wr, wi, wni = sc(PWR, R1 + s_), sc(PWI, R1 + s_), sc(PWN, R1 + s_)
                ops.append(lambda dst2=dst2, src2=src2, d=d, srck=srck, dstk=dstk: P.op(
                    "dve", "tensor_copy", out=dst2[:, :, 0:d], in_=src2[:, :, 0:d], reads=list(srck), writes=list(dstk)))
                n = J1 - d
                stt(dst[0][:, d:J1], src[0][:, 0:n], wr, src[0][:, d:J1], list(srck), [dstk[0]])
                stt(dst[1][:, d:J1], src[1][:, 0:n], wr, src[1][:, d:J1], list(srck), [dstk[1]])
                stt(dst[0][:, d:J1], src[1][:, 0:n], wni, dst[0][:, d:J1], list(srck) + [dstk[0]], [dstk[0]])
                stt(dst[1][:, d:J1], src[0][:, 0:n], wi, dst[1][:, d:J1], list(srck) + [dstk[1]], [dstk[1]])
                src, srck, src2 = dst, dstk, dst2
            if hf + 1 < NH:
                ops.append(lambda: P.op("dve", "tensor_copy", out=carry[:, k, :], in_=Xs[:, :, R1 - 1, J1 - 1],
                                        reads=list(klast), writes=[kc]))
            for r in range(R1 - 1):
                cacc(Xs[:, 0, r, 1:J1], Xs[:, 1, r, 1:J1], Xs[:, 0, R1 - 1, 0:J1 - 1], Xs[:, 1, R1 - 1, 0:J1 - 1],
                     r + 1, kx(xs, 0, r), kx(xs, 1, r), klast[0], klast[1])
            return ops

        def S3(u, rg):
            xs = u % 2
            rs_ = slice(rg * RG, (rg + 1) * RG)
            P.op("act", "activation", out=Xb[rg % 2][:], in_=X[xs][:, :, rs_, :], func=AF.Copy,
                 reads=[kx(xs, c, r) for c in range(2) for r in range(rg * RG, (rg + 1) * RG)]
                 + [kx(xs, 0, R1 - 1), kx(xs, 1, R1 - 1)], writes=[("C_Xb", rg % 2)])

        def S4_pe(u, r):
            m, hf, q4 = units[u]
            k = m * 4 + q4
            b = rot.next()
            s4banks[(u, r)] = b
            rg = r // RG
            for ri in range(2):
                P.op("pe", "matmul", out=ps[:, b, 0:RP * J1].rearrange("p (r j) -> p r j", r=RP),
                     lhsT=CT[:, k * 2 + ri, :], rhs=Xb[rg % 2][:, ri, r % RG:r % RG + RP, :],
                     start=(ri == 0), stop=(ri == 1), reads=ctf + [("C_Xb", rg % 2)], writes=[("ps", b)],
                     sig=(ri == 1))

        def S4_act(u, r):
            m, hf, q4 = units[u]
            b = s4banks.pop((u, r))
            P.op("act", "activation", out=yraw[q4 * 32:(q4 + 1) * 32, r:r + RP, :],
                 in_=ps[q4 * 32:(q4 + 1) * 32, b, 0:RP * J1].rearrange("p (r j) -> p r j", r=RP),
                 func=AF.Copy, reads=[("ps", b)], writes=[("C_yraw", q4, rr) for rr in range(r, r + RP)])

        gst = dict(rr=0)
        gslot = {}
        NGS = 4
        ta_ = ta_ + [sb("C_ga%d" % i, [128, J1], F32) for i in range(2, NGS)]
        tb_2 = tb_2 + [sb("C_gb%d" % i, [128, J1], F32) for i in range(2, NGS)]
        go = go + [sb("C_go%d" % i, [128, J1], BF16) for i in range(2, NGS)]

        def G_a(g4, r):
            m, hf = g4 // NH, g4 % NH
            us = g4 % 2
            i = gst["rr"] % NGS
            gst["rr"] += 1
            gslot[(g4, r)] = i
            P.op("dve", "scalar_tensor_tensor", out=ta_[i][:], in0=uTt[us][:, r::R1], scalar=Dcol[:, m:m + 1],
                 in1=yraw[:, r, :], op0=ALU.mult, op1=ALU.add,
                 reads=[("C_uTt", us), ("C_D",)] + [("C_yraw", q, r) for q in range(4)], writes=[("C_ga", i)])
            P.op("pool", "tensor_tensor", out=tb_2[i][:], in0=ta_[i][:], in1=ta_[i][:], op=ALU.mult,
                 reads=[("C_ga", i)], writes=[("C_gb", i)])
            P.op("pool", "tensor_scalar", out=tb_2[i][:], in0=tb_2[i][:], scalar1=0.044715, scalar2=1.0,
                 op0=ALU.mult, op1=ALU.add, reads=[("C_gb", i)], writes=[("C_gb", i)])
            P.op("pool", "tensor_tensor", out=tb_2[i][:], in0=tb_2[i][:], in1=ta_[i][:], op=ALU.mult,
                 reads=[("C_gb", i), ("C_ga", i)], writes=[("C_gb", i)])

        def G_b(g4, r):
            m, hf = g4 // NH, g4 % NH
            i = gslot.pop((g4, r))
            P.op("act", "activation", out=tb_2[i][:], in_=tb_2[i][:], func=AF.Tanh, scale=math.sqrt(2.0 / math.pi),
                 reads=[("C_gb", i)], writes=[("C_gb", i)])
            P.op("pool", "tensor_scalar", out=tb_2[i][:], in0=tb_2[i][:], scalar1=1.0, scalar2=0.5, op0=ALU.add,
                 op1=ALU.mult, reads=[("C_gb", i)], writes=[("C_gb", i)])
            P.op("pool", "tensor_tensor", out=go[i][:], in0=ta_[i][:], in1=tb_2[i][:], op=ALU.mult,
                 reads=[("C_ga", i), ("C_gb", i)], writes=[("C_go", i)])
            col = hf * TS + r * J1
            P.dma("sp", C.gT[m * 128:(m + 1) * 128, col:col + J1], go[i][:], reads=[("C_go", i)],
                  writes=[("gT", m, hf, r)])

        def G(g4, r):
            G_a(g4, r)
            G_b(g4, r)

        NG = 16

        def c_group(step, g, ops, per):
            def f():
                u, prev, nxt = step, step - 1, step + 1
                if prev >= 0:
                    if g < R1 // RG:
                        S3(prev, g)
                    if 1 <= g <= 4:
                        for r in range(4 * (g - 1), 4 * g, RP):
                            S4_pe(prev, r)
                            S4_act(prev, r)
                    if prev % 4 == 3 and 8 <= g <= 15:
                        for r in range(2 * (g - 8), 2 * (g - 7)):
                            G_b(prev // 4, r)
                    if prev % 4 == 3 and 7 <= g <= 14:
                        for r in range(2 * (g - 7), 2 * (g - 6)):
                            G_a(prev // 4, r)
                if nxt < NU:
                    if g == 0 and nxt % 4 == 0:
                        load_u(nxt // 4)
                    if 6 <= g <= 13:
                        for r in range(2 * (g - 6), 2 * (g - 5)):
                            if r % RP == 0:
                                S1_pe(nxt, r)
                                S1_act(nxt, r)
                for o in ops[g * per:(g + 1) * per]:
                    o()
            return f

        load_u(0)
        for r in range(0, R1, RP):
            S1_pe(0, r)
            S1_act(0, r)
        c_items = []
        import os
        if os.environ.get("BC_NOSKEW"):
            def unit_all(u):
                def f():
                    if u > 0:
                        if u % 4 == 0:
                            load_u(u // 4)
                        for r in range(0, R1, RP):
                            S1_pe(u, r)
                            S1_act(u, r)
                    for o in S2_gen(u):
                        o()
                    for rg in range(R1 // RG):
                        S3(u, rg)
                        for r in range(rg * RG, (rg + 1) * RG, RP):
                            S4_pe(u, r)
                            S4_act(u, r)
                    if u % 4 == 3:
                        for r in range(R1):
                            G(u // 4, r)
                return f
            c_items = [(1.0, unit_all(u)) for u in range(NU)]
        for step in (range(NU + 1) if not os.environ.get("BC_NOSKEW") else []):
            ops = S2_gen(step) if step < NU else []
            per = -(-len(ops) // NG) if ops else 0
            for g in range(NG):
                c_items.append((1.0, c_group(step, g, ops, per)))

        import os
        if os.environ.get("BC_SKIP_B"):
            b_items = []
        nb, ncn = len(b_items), len(c_items)
        ib = ic = 0
        while ib < nb or ic < ncn:
            if ic < ncn and (ib >= nb or ic * nb <= ib * ncn * 1.06):
                c_items[ic][1]()
                ic += 1
            else:
                b_items[ib][1]()
                ib += 1
        while pending:
            pending.pop(0)()
        P.barrier()
        P.flush()
    with ExitStack() as st:
        sb = lambda n, s, d: st.enter_context(nc.sbuf_tensor("%s_l%d" % (n, l), s, d))
        GW = sb("C_GW", [128, 4, 512], BF16)
        gb = sb("C_gbias", [128, 4], F32)
        gt = [sb("C_gt%d" % i, [128, 4, J1], BF16) for i in range(2)]
        sg = [sb("C_sg%d" % i, [128, 512], F32) for i in range(2)]
        SO = sb("C_SO", [128, 4, L], BF16)
        P.dma("pool", GW[:], C.glu_w[l].rearrange("(kt p) n -> p kt n", p=128), writes=[("C_GW",)])
        P.dma("sp", gb[:], C.glu_b[l].rearrange("(m p) -> p m", p=128), writes=[("C_gbias",)],
              allow_slow_non_contiguous=True)
        rr = 0
        for blk in range(NH * R1):
            hf, r = blk // R1, blk % R1
            i = blk % 2
            P.dma("sp", gt[i][:], C.gT[:, blk * J1:(blk + 1) * J1].rearrange("(m p) t -> p m t", p=128),
                  writes=[("C_gt", i)])
            for mo in range(4):
                b = rr % 4
                rr += 1
                for kt in range(4):
                    P.op("pe", "matmul", out=ps[:, b, 0:J1], lhsT=GW[:, kt, mo * 128:(mo + 1) * 128],
                         rhs=gt[i][:, kt, :], start=(kt == 0), stop=(kt == 3),
                         reads=[("C_GW",), ("C_gt", i)], writes=[("ps", b)], sig=(kt == 3))
                j = rr % 2
                P.op("act", "activation", out=sg[j][:, 0:J1], in_=ps[:, b, 0:J1], func=AF.Sigmoid,
                     bias=gb[:, mo:mo + 1], reads=[("ps", b), ("C_gbias",)], writes=[("C_sg", j)])
                P.op("dve", "tensor_tensor", out=SO[:, mo, hf * TS + r:(hf + 1) * TS:R1], in0=gt[i][:, mo, :],
                     in1=sg[j][:, 0:J1], op=ALU.mult, reads=[("C_gt", i), ("C_sg", j)], writes=[("C_SO", mo, blk)])
        for mo in range(4):
            for hf in range(2):
                sl = slice(hf * (L // 2), (hf + 1) * (L // 2))
                P.dma("sp", C.mixT[512 + mo * 128:512 + (mo + 1) * 128, sl], SO[:, mo, sl],
                      reads=[("C_SO", mo, blk) for blk in range(NH * R1)], writes=[("mixT", 4, mo, hf)])
        P.barrier()
        P.flush()


def resid_ln(P, C, tag, i, banks, xres, xres_keys, gt, bt, r, st6, mv, sd, out_ap, out_keys):
    ps = C.ps
    for nb in range(2):
        P.op("dve", "scalar_tensor_tensor", out=r[:, nb * 512:(nb + 1) * 512], in0=xres[:, nb * 512:(nb + 1) * 512],
             scalar=ALPHA, in1=ps[:, banks[nb], :], op0=ALU.mult, op1=ALU.add,
             reads=list(xres_keys) + [("ps", banks[nb])], writes=[(tag + "_r", i, nb)])
        P.op("dve", "bn_stats", out=st6[:, nb, :], in_=r[:, nb * 512:(nb + 1) * 512],
             reads=[(tag + "_r", i, nb)], writes=[(tag + "_st", i, nb)])
    P.op("dve", "bn_aggr", out=mv[:, 0:2], in_=st6[:].rearrange("p a b -> p (a b)"),
         reads=[(tag + "_st", i, 0), (tag + "_st", i, 1)], writes=[(tag + "_mv", i)])
    P.op("act", "activation", out=sd[:, 0:1], in_=mv[:, 1:2], func=AF.Sqrt, bias=C.eps_col[:, 1:2],
         reads=[(tag + "_mv", i), ("eps_col",)], writes=[(tag + "_sd", i)])
    P.op("dve", "reciprocal", out=sd[:, 1:2], in_=sd[:, 0:1], reads=[(tag + "_sd", i)], writes=[(tag + "_rstd", i)])
    rk = [(tag + "_r", i, 0), (tag + "_r", i, 1)]
    P.op("dve", "scalar_tensor_tensor", out=r[:], in0=r[:], scalar=mv[:, 0:1], in1=gt[:], op0=ALU.subtract,
         op1=ALU.mult, reads=rk + [(tag + "_mv", i), (tag + "_g",)], writes=rk)
    P.op("dve", "scalar_tensor_tensor", out=out_ap, in0=r[:], scalar=sd[:, 1:2], in1=bt[:], op0=ALU.mult,
         op1=ALU.add, reads=rk + [(tag + "_rstd", i), (tag + "_b",)], writes=list(out_keys))


def phase_D1(P, nc, C, l, x_src):
    L = C.L
    NMT = L // 512
    ps, psb = C.ps, C.psb
    with ExitStack() as st:
        sb = lambda n, s, d: st.enter_context(nc.sbuf_tensor("%s_l%d" % (n, l), s, d))
        Wo = sb("D1_w", [128, 8, 1024], BF16)
        gt = sb("D1_g", [128, 1024], F32)
        bt = sb("D1_b", [128, 1024], F32)
        mixt = [sb("D1_mix%d" % i, [128, 8, 512], BF16) for i in range(2)]
        xr = [sb("D1_xr%d" % i, [128, 4, 1024], F32) for i in range(2)]
        x1o = [sb("D1_x1o%d" % i, [128, 4, 1024], F32) for i in range(2)]
        r = [sb("D1_r%d" % i, [128, 1024], F32) for i in range(4)]
        x1b = [sb("D1_x1b%d" % i, [128, 1024], BF16) for i in range(4)]
        x1T = [sb("D1_x1T%d" % i, [128, 8, 512], BF16) for i in range(2)]
        st6 = [sb("D1_st%d" % i, [128, 2, 6], F32) for i in range(4)]
        mv = [sb("D1_mv%d" % i, [128, 2], F32) for i in range(4)]
        sd = [sb("D1_sd%d" % i, [128, 2], F32) for i in range(4)]
        wosrc = C.w_out[l].rearrange("(kt p) n -> p kt n", p=128)
        for cb in range(2):
            P.dma("pool", Wo[:, :, cb * 512:(cb + 1) * 512], wosrc[:, :, cb * 512:(cb + 1) * 512],
                  writes=[("D1_w", cb)])
        P.dma("sp", gt[:].unsqueeze(1), C.ln1_g[l:l + 1, :].partition_broadcast(128), writes=[("D1_g",)])
        P.dma("sp", bt[:].unsqueeze(1), C.ln1_b[l:l + 1, :].partition_broadcast(128), writes=[("D1_b",)])

        def load(mt):
            s = mt % 2
            sl = slice(mt * 512, (mt + 1) * 512)
            P.dma("sp", mixt[s][:], C.mixT[:, sl].rearrange("(kt p) t -> p kt t", p=128),
                  writes=[("D1_mix", s)])
            P.dma("sp", xr[s][:], x_src[sl, :].rearrange("(s p) d -> p s d", p=128),
                  reads=[("xs", 2 * mt), ("xs", 2 * mt + 1)], writes=[("D1_xr", s)])

        load(0)
        cnt = dict(mb=0, tbr=0)
        tiles = [(mt, sub) for mt in range(NMT) for sub in range(4)]
        NT_ = len(tiles)

        def stage1(t):
            mt, sub = tiles[t]
            s = mt % 2
            i = t % 4
            if sub == 0 and mt + 1 < NMT:
                load(mt + 1)
            banks = []
            for nb in range(2):
                b = 2 + cnt["mb"] % 6
                cnt["mb"] += 1
                banks.append(b)
                for kt in range(8):
                    P.op("pe", "matmul", out=ps[:, b, :], lhsT=mixt[s][:, kt, sub * 128:(sub + 1) * 128],
                         rhs=Wo[:, kt, nb * 512:(nb + 1) * 512], start=(kt == 0), stop=(kt == 7),
                         reads=[("D1_mix", s), ("D1_w", nb)], writes=[("ps", b)], sig=(kt == 7))
            resid_ln(P, C, "D1", i, banks, xr[s][:, sub, :], [("D1_xr", s)], gt, bt, r[i], st6[i], mv[i], sd[i],
                     x1o[s][:, sub, :], [("D1_x1o", s, sub)])
            P.op("act", "activation", out=x1b[i][:], in_=x1o[s][:, sub, :], func=AF.Copy,
                 reads=[("D1_x1o", s, sub)], writes=[("D1_x1b", i)])

        def stage2(t):
            mt, sub = tiles[t]
            s = mt % 2
            i = t % 4
            tb = cnt["tbr"] % 2
            cnt["tbr"] += 1
            for kt in range(8):
                P.op("pe", "transpose", out=psb[:, tb, kt * 128:(kt + 1) * 128],
                     in_=x1b[i][:, kt * 128:(kt + 1) * 128], identity=C.ident_b[:],
                     reads=[("D1_x1b", i), ("ident_b",)], writes=[("psT", tb)], sig=(kt == 7))
            P.op("act", "activation", out=x1T[s][:, :, sub * 128:(sub + 1) * 128],
                 in_=psb[:, tb, :].rearrange("p (k t) -> p k t", k=8), func=AF.Copy,
                 reads=[("psT", tb)], writes=[("D1_x1T", s, sub)])
            if sub == 3:
                sl = slice(mt * 512, (mt + 1) * 512)
                P.dma("sp", C.x1[sl, :].rearrange("(s p) d -> p s d", p=128), x1o[s][:],
                      reads=[("D1_x1o", s, q) for q in range(4)], writes=[("x1", 2 * mt), ("x1", 2 * mt + 1)])
                P.dma("sp", C.x1T[:, sl].rearrange("(kt p) t -> p kt t", p=128), x1T[s][:],
                      reads=[("D1_x1T", s, q) for q in range(4)], writes=[("x1T", 2 * mt), ("x1T", 2 * mt + 1)])

        LOOK = 2
        for t in range(min(LOOK, NT_)):
            stage1(t)
        for t in range(NT_):
            if t + LOOK < NT_:
                stage1(t + LOOK)
            stage2(t)
        P.barrier()
        P.flush()


def phase_D2(P, nc, C, l, out_dst, out_tag):
    L = C.L
    NT2 = L // 256
    ps = C.ps
    with ExitStack() as st:
        sb = lambda n, s, d: st.enter_context(nc.sbuf_tensor("%s_l%d" % (n, l), s, d))
        W1 = sb("D2_w1", [128, 8, 4096], BF16)
        W2 = sb("D2_w2", [128, 32, 1024], BF16)
        gt = sb("D2_g", [128, 1024], F32)
        bt = sb("D2_b", [128, 1024], F32)
        xT = [sb("D2_xT%d" % i, [128, 8, 256], BF16) for i in range(2)]
        xr = [sb("D2_xr%d" % i, [128, 2, 1024], F32) for i in range(2)]
        hT = sb("D2_hT", [128, 32, 256], BF16)
        rl = [sb("D2_rl%d" % i, [128, 512], BF16) for i in range(2)]
        r = [sb("D2_r%d" % i, [128, 1024], F32) for i in range(2)]
        yo = [sb("D2_yo%d" % i, [128, 1024], F32) for i in range(2)]
        st6 = [sb("D2_st%d" % i, [128, 2, 6], F32) for i in range(2)]
        mv = [sb("D2_mv%d" % i, [128, 2], F32) for i in range(2)]
        sd = [sb("D2_sd%d" % i, [128, 2], F32) for i in range(2)]
        w1src = C.w_ff1[l].rearrange("(kt p) n -> p kt n", p=128)
        for cb in range(8):
            P.dma("pool", W1[:, :, cb * 512:(cb + 1) * 512], w1src[:, :, cb * 512:(cb + 1) * 512],
                  writes=[("D2_w1", cb)])
        for kt in range(32):
            P.dma("pool", W2[:, kt, :], C.w_ff2[l, kt * 128:(kt + 1) * 128, :], writes=[("D2_w2", kt)])
        P.dma("sp", gt[:].unsqueeze(1), C.ln2_g[l:l + 1, :].partition_broadcast(128), writes=[("D2_g",)])
        P.dma("sp", bt[:].unsqueeze(1), C.ln2_b[l:l + 1, :].partition_broadcast(128), writes=[("D2_b",)])
        w1k = [("D2_w1", kt) for kt in range(8)]

        def load(t2):
            s = t2 % 2
            sl = slice(t2 * 256, (t2 + 1) * 256)
            P.dma("sp", xT[s][:], C.x1T[:, sl].rearrange("(kt p) t -> p kt t", p=128),
                  reads=[("x1T", t2)], writes=[("D2_xT", s)])
            P.dma("sp", xr[s][:], C.x1[sl, :].rearrange("(s p) d -> p s d", p=128),
                  reads=[("x1", t2)], writes=[("D2_xr", s)])

        load(0)
        ub = 0
        it = 0
        for t2 in range(NT2):
            s = t2 % 2
            if t2 + 1 < NT2:
                load(t2 + 1)
            for fp in range(16):
                b = ub % 4
                j = ub % 2
                ub += 1
                for half in range(2):
                    ft = fp * 2 + half
                    for kt in range(8):
                        P.op("pe", "matmul", out=ps[:, b, half * 256:(half + 1) * 256],
                             lhsT=W1[:, kt, ft * 128:(ft + 1) * 128], rhs=xT[s][:, kt, :],
                             start=(kt == 0), stop=(kt == 7), reads=[("D2_xT", s), ("D2_w1", ft // 4)],
                             writes=[("ps", b)], sig=(kt == 7 and half == 1))
                P.op("act", "activation", out=rl[j][:], in_=ps[:, b, :], func=AF.Relu,
                     reads=[("ps", b)], writes=[("D2_rl", j)])
                P.op("dve", "tensor_tensor", out=hT[:, fp * 2:fp * 2 + 2, :],
                     in0=rl[j][:].rearrange("p (a t) -> p a t", a=2), in1=rl[j][:].rearrange("p (a t) -> p a t", a=2),
                     op=ALU.mult, reads=[("D2_rl", j)], writes=[("D2_hT", fp)])
            hk = [("D2_hT", fp) for fp in range(16)]
            for sub in range(2):
                i = it % 2
                it += 1
                banks = [4 + sub * 2, 5 + sub * 2]
                for nb in range(2):
                    b = banks[nb]
                    for kt in range(32):
                        P.op("pe", "matmul", out=ps[:, b, :], lhsT=hT[:, kt, sub * 128:(sub + 1) * 128],
                             rhs=W2[:, kt, nb * 512:(nb + 1) * 512], start=(kt == 0), stop=(kt == 31),
                             reads=hk + [("D2_w2", kt)], writes=[("ps", b)], sig=(kt == 31))
                resid_ln(P, C, "D2", i, banks, xr[s][:, sub, :], [("D2_xr", s)], gt, bt, r[i], st6[i], mv[i], sd[i],
                         yo[i][:], [("D2_yo", i)])
                row = t2 * 256 + sub * 128
                P.dma("sp", out_dst[row:row + 128, :], yo[i][:], reads=[("D2_yo", i)], writes=[(out_tag, t2)])
        P.barrier()
        P.flush()


def build(L=8192, phases=None, dbg=()):
    nc = bass.Bass("TRN2", target_bir_lowering=False)
    C = declare_dram(nc, L, dbg)
    with ExitStack() as st:
        P = Prog(nc, st)
        C.ps_t = st.enter_context(nc.psum_tensor("ps", [128, 8, 512], F32))
        C.ps = C.ps_t
        C.psb = C.ps_t[:].bitcast(BF16)
        C.ident_b = st.enter_context(nc.sbuf_tensor("ident_b", [128, 128], BF16))
        C.tri_b = st.enter_context(nc.sbuf_tensor("tri_b", [128, 128], BF16))
        C.ones_b = st.enter_context(nc.sbuf_tensor("ones_b", [128, 128], BF16))
        C.ident_f = st.enter_context(nc.sbuf_tensor("ident_f", [128, 128], F32))
        P.dma("pool", C.ident_b[:], C.c_misc[0], writes=[("ident_b",)])
        P.dma("pool", C.tri_b[:], C.c_misc[1], writes=[("tri_b",)])
        P.dma("pool", C.ones_b[:], C.c_misc[2], writes=[("ones_b",)])
        P.dma("sp", C.ident_f[:], C.c_misc[0], writes=[("ident_f",)])
        C.eps_col = st.enter_context(nc.sbuf_tensor("eps_col", [128, 4], F32))
        P.op("pool", lambda e: e.memset(C.eps_col[:, 0:1], RMS_EPS), writes=[("eps_col",)])
        P.op("pool", lambda e: e.memset(C.eps_col[:, 1:2], LN_EPS), writes=[("eps_col",)])
        P.barrier()
        P.flush()
        for l in range(DEPTH):
            x_src = C.x if l == 0 else C.xs
            if phases is None or ("A", l) in phases:
                phase_A(P, nc, C, l, x_src)
            if phases is None or ("BC", l) in phases:
                phase_BC(P, nc, C, l)
            if phases is not None and ("B", l) in phases:
                phase_B(P, nc, C, l)
            if phases is not None and ("C", l) in phases:
                phase_C(P, nc, C, l)
            if phases is None or ("D1", l) in phases:
                phase_D1(P, nc, C, l, x_src)
            if phases is None or ("D2", l) in phases:
                phase_D2(P, nc, C, l, C.xs if l == 0 else C.y, "xs" if l == 0 else "y")
    return nc


_NC_CACHE = {}
_IN_NAMES = ["w_in", "w_out", "lam_qk", "subln_g", "ssm_lam_re", "ssm_lam_im", "ssm_log_dt", "ssm_b_re",
             "ssm_b_im", "ssm_c_re", "ssm_c_im", "ssm_d", "glu_w", "glu_b", "ln1_g", "ln1_b", "w_ff1", "w_ff2",
             "ln2_g", "ln2_b"]


def kernel(**inputs):
    x = np.ascontiguousarray(np.asarray(inputs["x"], dtype=np.float32))
    B, L, D = x.shape
    if L not in _NC_CACHE:
        _NC_CACHE[L] = build(L)
    nc = _NC_CACHE[L]
    consts = make_consts(L)
    shared = {k: np.ascontiguousarray(np.asarray(inputs[k], dtype=np.float32)) for k in _IN_NAMES}
    shared.update(consts)
    in_maps = []
    for b in range(B):
        m = dict(shared)
        m["x"] = x[b]
        in_maps.append(m)
    res = run_bass_kernel_spmd(nc, in_maps, core_ids=list(range(B)))
    out = np.stack([np.asarray(r["y"], dtype=np.float32).reshape(L, D) for r in res.results], axis=0)
    return out
```
